# Optimizing a Trainium2 kernel written in Bass

```python
import math
import jax
import jax.numpy as jnp
from jax import lax
import numpy as np

D_MODEL = 1024
BATCH = 8
SEQ = 4096
DEPTH = 4

HEAD_DIM = 64
CONV_CH = 256
CONV_WIDTH = 31
N_Q_HEADS = 12
N_KV_HEADS = 4
Q_PER_KV = N_Q_HEADS // N_KV_HEADS
NSA_WIDTH = N_Q_HEADS * HEAD_DIM
MIX_WIDTH = CONV_CH + NSA_WIDTH
KV_WIDTH = N_KV_HEADS * HEAD_DIM
CMP_LEN = 32
CMP_STRIDE = 16
CMP_HIDDEN = 256
SEL_BLOCK = 64
SEL_TOPK = 8
WINDOW = 512
Q_BLOCK = 64
N_BRANCH = 3
FORCED_BONUS = 1000.0
D_FF = -(-8 * D_MODEL // (3 * 256)) * 256
IN_COLS = 2 * CONV_CH + NSA_WIDTH + 6 * KV_WIDTH + N_BRANCH * N_Q_HEADS
EPS = 1e-6

kernel_name = "hymba_conformer_nsa_alibi_trunk"


def rms_norm(x, g):
    xf = x.astype(jnp.float32)
    y = xf * lax.rsqrt(jnp.mean(xf * xf, axis=-1, keepdims=True) + EPS)
    return (y * g.astype(jnp.float32)).astype(x.dtype)


def layer_norm(x, g, b):
    xf = x.astype(jnp.float32)
    mu = jnp.mean(xf, axis=-1, keepdims=True)
    var = jnp.mean(jnp.square(xf - mu), axis=-1, keepdims=True)
    y = (xf - mu) * lax.rsqrt(var + EPS)
    return (y * g.astype(jnp.float32) + b.astype(jnp.float32)).astype(x.dtype)


def alibi_slopes(n):
    def pow2_slopes(m):
        start = 2.0 ** (-8.0 / m)
        return [start ** (i + 1) for i in range(m)]
    if math.log2(n).is_integer():
        s = pow2_slopes(n)
    else:
        c = 2 ** math.floor(math.log2(n))
        s = pow2_slopes(c) + pow2_slopes(2 * c)[0::2][: n - c]
    return np.asarray(s, dtype=np.float32)


def masked_softmax(s, mask):
    s = jnp.where(mask, s.astype(jnp.float32), -jnp.inf)
    m = jnp.max(s, axis=-1, keepdims=True)
    m = jnp.where(jnp.isfinite(m), m, 0.0)
    p = jnp.exp(s - m)
    return p / jnp.maximum(jnp.sum(p, axis=-1, keepdims=True), 1e-30)


def conv_mixer(a, w_dw, b_dw, ln_g, ln_b):
    u, v = jnp.split(a, 2, axis=-1)
    y = u * jax.nn.sigmoid(v)
    y = lax.conv_general_dilated(
        y, w_dw[:, None, :], window_strides=(1,),
        padding=[(CONV_WIDTH - 1, 0)],
        dimension_numbers=("NWC", "WIO", "NWC"),
        feature_group_count=CONV_CH) + b_dw
    y = layer_norm(y, ln_g, ln_b)
    return jax.nn.silu(y)


def compress(kr, pe, w1, w2):
    b, t = kr.shape[:2]
    chunks = kr.reshape(b, t // CMP_STRIDE, CMP_STRIDE, N_KV_HEADS, HEAD_DIM)
    lo = jnp.einsum("bclgd,ldh->bcgh", chunks, w1[:CMP_STRIDE])
    hi = jnp.einsum("bclgd,ldh->bcgh", chunks, w1[CMP_STRIDE:])
    h = lo[:, :-1] + hi[:, 1:] + jnp.einsum("ld,ldh->h", pe, w1)
    return jnp.einsum("bngh,hd->bngd", jax.nn.silu(h), w2)


def nsa_attention(q, kc, vc, ks, vs, kw, vw, gates):
    b, t = q.shape[:2]
    n_cmp = kc.shape[1]
    n_sel = t // SEL_BLOCK
    k_top = min(SEL_TOPK, n_sel)
    scale = HEAD_DIM ** -0.5
    slopes = jnp.asarray(alibi_slopes(N_Q_HEADS)).reshape(N_KV_HEADS, Q_PER_KV)[None, :, :, None, None]
    cmp_idx = jnp.arange(n_cmp)
    cmp_start = cmp_idx * CMP_STRIDE
    cmp_center = cmp_start.astype(jnp.float32) + 0.5 * (CMP_LEN - 1)
    cmp_end = cmp_start + CMP_LEN - 1
    sel_start = jnp.arange(n_sel) * SEL_BLOCK
    overlap = ((cmp_start[:, None] < sel_start[None, :] + SEL_BLOCK)
               & (cmp_start[:, None] + CMP_LEN > sel_start[None, :])).astype(jnp.float32)
    ks_blk = ks.reshape(b, n_sel, SEL_BLOCK, N_KV_HEADS, HEAD_DIM).transpose(0, 3, 1, 2, 4)
    vs_blk = vs.reshape(b, n_sel, SEL_BLOCK, N_KV_HEADS, HEAD_DIM).transpose(0, 3, 1, 2, 4)
    pad = ((0, 0), (WINDOW, 0), (0, 0), (0, 0))
    kw_pad = jnp.pad(kw, pad)
    vw_pad = jnp.pad(vw, pad)
    b_idx = jnp.arange(b)[:, None, None, None]
    g_idx = jnp.arange(N_KV_HEADS)[None, :, None, None]
    offs = jnp.arange(SEL_BLOCK)
    win_offs = jnp.arange(WINDOW + Q_BLOCK) - WINDOW
    sel_j = jnp.arange(n_sel)

    def block(i):
        t0 = i * Q_BLOCK
        tq = t0 + jnp.arange(Q_BLOCK)
        tqf = tq.astype(jnp.float32)
        qb = lax.dynamic_slice_in_dim(q, t0, Q_BLOCK, axis=1) * scale
        gb = lax.dynamic_slice_in_dim(gates, t0, Q_BLOCK, axis=1)

        s_c = jnp.einsum("bqgrd,bngd->bgrqn", qb, kc) - slopes * (tqf[:, None] - cmp_center[None, :])
        p_c = masked_softmax(s_c, cmp_end[None, :] <= tq[:, None])
        o_c = jnp.einsum("bgrqn,bngd->bqgrd", p_c, vc)

        imp = jnp.einsum("bgrqn,nj->bgqj", p_c, overlap)
        cur = tq // SEL_BLOCK
        forced = ((sel_j[None, :] == 0) | (sel_j[None, :] == cur[:, None])
                  | (sel_j[None, :] == cur[:, None] - 1))
        imp = jnp.where(sel_j[None, :] <= cur[:, None],
                        imp + FORCED_BONUS * forced.astype(jnp.float32), -jnp.inf)
        top_s, top_i = lax.top_k(imp, k_top)

        kg = ks_blk[b_idx, g_idx, top_i]
        vg = vs_blk[b_idx, g_idx, top_i]
        pos = top_i[..., None] * SEL_BLOCK + offs
        mask_s = jnp.isfinite(top_s)[..., None] & (pos <= tq[None, None, :, None, None])
        dist_s = (tqf[None, None, :, None, None] - pos.astype(jnp.float32))[:, :, None]
        s_s = jnp.einsum("bqgrd,bgqksd->bgrqks", qb, kg) - slopes[..., None] * dist_s
        s_s = s_s.reshape(b, N_KV_HEADS, Q_PER_KV, Q_BLOCK, k_top * SEL_BLOCK)
        p_s = masked_softmax(s_s, mask_s.reshape(b, N_KV_HEADS, 1, Q_BLOCK, k_top * SEL_BLOCK))
        o_s = jnp.einsum("bgrqs,bgqsd->bqgrd", p_s,
                         vg.reshape(b, N_KV_HEADS, Q_BLOCK, k_top * SEL_BLOCK, HEAD_DIM))

        kwb = lax.dynamic_slice_in_dim(kw_pad, t0, WINDOW + Q_BLOCK, axis=1)
        vwb = lax.dynamic_slice_in_dim(vw_pad, t0, WINDOW + Q_BLOCK, axis=1)
        pos_w = t0 + win_offs
        dist_w = tq[:, None] - pos_w[None, :]
        mask_w = (dist_w >= 0) & (dist_w < WINDOW) & (pos_w[None, :] >= 0)
        s_w = jnp.einsum("bqgrd,bkgd->bgrqk", qb, kwb) - slopes * dist_w.astype(jnp.float32)
        p_w = masked_softmax(s_w, mask_w)
        o_w = jnp.einsum("bgrqk,bkgd->bqgrd", p_w, vwb)

        out = gb[..., 0:1] * o_c + gb[..., 1:2] * o_s + gb[..., 2:3] * o_w
        return out.astype(q.dtype)

    outs = lax.map(block, jnp.arange(t // Q_BLOCK))
    return jnp.moveaxis(outs, 0, 1).reshape(b, t, NSA_WIDTH)


def hybrid_layer(x, attn_norm, w_in, conv_w, conv_b, conv_ln_g, conv_ln_b,
                 cmp_k_pe, cmp_k_w1, cmp_k_w2, cmp_v_pe, cmp_v_w1, cmp_v_w2,
                 w_out, ffn_norm, w_gate_up, w_down):
    b, t, _ = x.shape
    h = rms_norm(x, attn_norm)
    z = h @ w_in
    splits = np.cumsum([2 * CONV_CH, NSA_WIDTH] + [KV_WIDTH] * 6).tolist()
    a_conv, q, kc_r, vc_r, ks, vs, kw, vw, g = jnp.split(z, splits, axis=-1)
    kv_shape = (b, t, N_KV_HEADS, HEAD_DIM)
    conv_out = conv_mixer(a_conv, conv_w, conv_b, conv_ln_g, conv_ln_b)
    kc = compress(kc_r.reshape(kv_shape), cmp_k_pe, cmp_k_w1, cmp_k_w2)
    vc = compress(vc_r.reshape(kv_shape), cmp_v_pe, cmp_v_w1, cmp_v_w2)
    gates = jax.nn.sigmoid(g).reshape(b, t, N_KV_HEADS, Q_PER_KV, N_BRANCH)
    nsa_out = nsa_attention(q.reshape(b, t, N_KV_HEADS, Q_PER_KV, HEAD_DIM), kc, vc,
                            ks.reshape(kv_shape), vs.reshape(kv_shape),
                            kw.reshape(kv_shape), vw.reshape(kv_shape), gates)
    mix = jnp.concatenate([conv_out, nsa_out.astype(conv_out.dtype)], axis=-1)
    x = x + mix @ w_out
    h = rms_norm(x, ffn_norm)
    gu = h @ w_gate_up
    gate, up = jnp.split(gu, 2, axis=-1)
    return x + (jax.nn.silu(gate) * up) @ w_down


def setup_inputs(seed: int = 0) -> dict:
    key = jax.random.key(seed)
    ks = jax.random.split(key, 20)
    f32 = jnp.float32

    def nrm(k, shape, scale):
        return jax.random.normal(k, shape, f32) * scale

    return {
        "x": nrm(ks[0], (BATCH, SEQ, D_MODEL), 1.0),
        "attn_norm": 1.0 + nrm(ks[1], (DEPTH, D_MODEL), 0.05),
        "w_in": nrm(ks[2], (DEPTH, D_MODEL, IN_COLS), D_MODEL ** -0.5),
        "conv_w": nrm(ks[3], (DEPTH, CONV_WIDTH, CONV_CH), CONV_WIDTH ** -0.5),
        "conv_b": nrm(ks[4], (DEPTH, CONV_CH), 0.01),
        "conv_ln_g": 1.0 + nrm(ks[5], (DEPTH, CONV_CH), 0.05),
        "conv_ln_b": nrm(ks[6], (DEPTH, CONV_CH), 0.01),
        "cmp_k_pe": nrm(ks[7], (DEPTH, CMP_LEN, HEAD_DIM), 0.1),
        "cmp_k_w1": nrm(ks[8], (DEPTH, CMP_LEN, HEAD_DIM, CMP_HIDDEN), (CMP_LEN * HEAD_DIM) ** -0.5),
        "cmp_k_w2": nrm(ks[9], (DEPTH, CMP_HIDDEN, HEAD_DIM), CMP_HIDDEN ** -0.5),
        "cmp_v_pe": nrm(ks[10], (DEPTH, CMP_LEN, HEAD_DIM), 0.1),
        "cmp_v_w1": nrm(ks[11], (DEPTH, CMP_LEN, HEAD_DIM, CMP_HIDDEN), (CMP_LEN * HEAD_DIM) ** -0.5),
        "cmp_v_w2": nrm(ks[12], (DEPTH, CMP_HIDDEN, HEAD_DIM), CMP_HIDDEN ** -0.5),
        "w_out": nrm(ks[13], (DEPTH, MIX_WIDTH, D_MODEL), MIX_WIDTH ** -0.5),
        "ffn_norm": 1.0 + nrm(ks[14], (DEPTH, D_MODEL), 0.05),
        "w_gate_up": nrm(ks[15], (DEPTH, D_MODEL, 2 * D_FF), D_MODEL ** -0.5),
        "w_down": nrm(ks[16], (DEPTH, D_FF, D_MODEL), D_FF ** -0.5),
        "final_norm": 1.0 + nrm(ks[17], (D_MODEL,), 0.05),
    }


def reference(x, attn_norm, w_in, conv_w, conv_b, conv_ln_g, conv_ln_b,
              cmp_k_pe, cmp_k_w1, cmp_k_w2, cmp_v_pe, cmp_v_w1, cmp_v_w2,
              w_out, ffn_norm, w_gate_up, w_down, final_norm):
    for l in range(DEPTH):
        x = hybrid_layer(x, attn_norm[l], w_in[l], conv_w[l], conv_b[l], conv_ln_g[l], conv_ln_b[l],
                         cmp_k_pe[l], cmp_k_w1[l], cmp_k_w2[l], cmp_v_pe[l], cmp_v_w1[l], cmp_v_w2[l],
                         w_out[l], ffn_norm[l], w_gate_up[l], w_down[l])
    return rms_norm(x, final_norm)
```

```python
import contextlib
import math
import numpy as np
import concourse.bass as bass
import concourse.mybir as mybir
from concourse.bass_utils import run_bass_kernel_spmd

F32 = mybir.dt.float32
BF16 = mybir.dt.bfloat16
AF = mybir.ActivationFunctionType
ALU = mybir.AluOpType

D = 1024
T = 4096
DEPTH = 4
HD = 64
CC = 256
CW = 31
NQ = 12
NKV = 4
RPG = 3
DFF = 2816
INC = 2852
EPS = 1e-6
NT = 512
NTT = T // NT
KC = D // 128
FC = DFF // 128
BIG = 30000.0
NSM = 84


def alibi_slopes(n):
    def p2(m):
        start = 2.0 ** (-8.0 / m)
        return [start ** (i + 1) for i in range(m)]
    if math.log2(n).is_integer():
        s = p2(n)
    else:
        c = 2 ** math.floor(math.log2(n))
        s = p2(c) + p2(2 * c)[0::2][: n - c]
    return [float(np.float32(v)) for v in s]


SLOPES = alibi_slopes(NQ)
ALIBI_CUT = 300.0


def sel_first_tile(h, i):
    kt0 = 0
    for kt in range(4 * i):
        dmin = 128 * (4 * i - kt) - 127
        if SLOPES[h] * dmin > ALIBI_CUT:
            kt0 = kt + 1
    return kt0
CC_DELTAS = [31, -481, -993, -1505, -2017]


class Sem:
    __slots__ = ("h",)

    def __init__(self, h):
        self.h = h


class Buf:
    __slots__ = ("w", "r")

    def __init__(self):
        self.w = {}
        self.r = {}


class Slot:
    __slots__ = ("sem", "val")

    def __init__(self, sem):
        self.sem = sem
        self.val = 0


class Eng:
    def __init__(self, kb, name, h, nslots=0):
        self.kb = kb
        self.name = name
        self.h = h
        self.sem = kb.newsem(name)
        self.cnt = 0
        self.seen = {}
        self.slots = [Slot(kb.newsem(f"{name}d{i}")) for i in range(nslots)]
        self.rr = 0


class KB:
    def __init__(self, nc, es):
        self.nc = nc
        self.es = es
        self.nsem = 0
        self.pe = Eng(self, "pe", nc.tensor)
        self.act = Eng(self, "act", nc.scalar)
        self.dve = Eng(self, "dve", nc.vector)
        self.pool = Eng(self, "pool", nc.gpsimd, nslots=8)
        self.sp = Eng(self, "sp", nc.sync, nslots=12)
        self.engs = [self.pe, self.act, self.dve, self.pool, self.sp]
        self.ninst = 0

    def newsem(self, name):
        self.nsem += 1
        return Sem(self.es.enter_context(self.nc.semaphore(f"s{self.nsem}_{name}")))

    def _wait(self, eng, deps):
        for sem, val in deps.items():
            if val > 0 and eng.seen.get(sem, 0) < val:
                eng.h.wait_ge(sem.h, val)
                eng.seen[sem] = val

    def _deps(self, eng, reads, writes):
        d = {}
        for b in reads:
            for s, v in b.w.items():
                if d.get(s, 0) < v:
                    d[s] = v
        for b in writes:
            for s, v in b.w.items():
                if d.get(s, 0) < v:
                    d[s] = v
            for s, v in b.r.items():
                if d.get(s, 0) < v:
                    d[s] = v
        if eng is self.pe:
            d.pop(self.pe.sem, None)
        return d

    def op(self, eng, fn, reads=(), writes=()):
        self._wait(eng, self._deps(eng, reads, writes))
        ins = fn()
        eng.cnt += 1
        self.ninst += 1
        ins.then_inc(eng.sem.h, 1)
        for b in reads:
            if b.r.get(eng.sem, 0) < eng.cnt:
                b.r[eng.sem] = eng.cnt
        for b in writes:
            b.w = {eng.sem: eng.cnt}
            b.r = {}

    def dma(self, q, out, in_, reads=(), writes=(), **kw):
        slot = q.slots[q.rr % len(q.slots)]
        q.rr += 1
        d = self._deps(q, reads, writes)
        if slot.val > 0 and d.get(slot.sem, 0) < slot.val:
            d[slot.sem] = slot.val
        self._wait(q, d)
        slot.val += 16
        self.ninst += 1
        q.h.dma_start(out=out, in_=in_, **kw).then_inc(slot.sem.h, 16)
        for b in reads:
            if b.r.get(slot.sem, 0) < slot.val:
                b.r[slot.sem] = slot.val
        for b in writes:
            b.w = {slot.sem: slot.val}
            b.r = {}

    def barrier(self):
        d = {}
        for e in self.engs:
            if e.cnt > 0:
                d[e.sem] = e.cnt
            for s in e.slots:
                if s.val > 0:
                    d[s.sem] = s.val
        for e in self.engs:
            self._wait(e, dict(d))
        for e in self.engs:
            if e.cnt > 30000:
                e.sem = self.newsem(e.name)
                e.cnt = 0


class Ph:
    _n = [0]

    def __init__(self, kb):
        self.kb = kb
        self.es = contextlib.ExitStack()
        Ph._n[0] += 1
        self.pfx = f"p{Ph._n[0]}_"

    def __enter__(self):
        self.es.__enter__()
        return self

    def __exit__(self, *a):
        self.kb.barrier()
        return self.es.__exit__(*a)

    def sb(self, name, shape, dt=F32):
        return self.es.enter_context(self.kb.nc.sbuf_tensor(self.pfx + name, list(shape), dt))

    def ps(self, name, shape, dt=F32):
        return self.es.enter_context(self.kb.nc.psum_tensor(self.pfx + name, list(shape), dt))


def host_constants():
    c = {}
    k = np.arange(128)[:, None]
    q = np.arange(512)[None, :]
    masks = []
    for v in range(4):
        masks.append(np.where(128 * v + k > q, -BIG, 0.0))
    for v in range(4):
        masks.append(np.where(q - k < 128 * v, 0.0, -BIG))
    for dl in CC_DELTAS:
        masks.append(np.where(16 * k + dl > q, -BIG, 0.0))
    c["c_mask"] = np.stack(masks, axis=1).astype(np.float32)
    pos = np.arange(T)
    c["c_E"] = (pos[None, :] // 64 == np.arange(64)[:, None]).astype(np.float32)
    sl = np.asarray(SLOPES, dtype=np.float64)
    dl = np.arange(-31, 4)
    b1 = sl[None, :, None] * (np.arange(128)[:, None, None] + 128.0 * dl[None, None, :])
    kt = np.arange(2)
    it = np.arange(8)
    b2 = sl[None, :, None, None] * (16.0 * np.arange(128)[:, None, None, None] + 15.5
                                    + 2048.0 * kt[None, None, :, None] - 512.0 * it[None, None, None, :])
    c["c_bias"] = np.concatenate([b1.reshape(128, -1), b2.reshape(128, -1)], axis=1).astype(np.float32)
    c["c_iota"] = np.broadcast_to(np.arange(512, dtype=np.float32)[None, :], (128, 512)).copy()
    n = np.arange(256)
    j = np.arange(64)
    cs = n[:, None] * 16
    ov = ((cs < j[None, :] * 64 + 64) & (cs + 32 > j[None, :] * 64)).astype(np.float32)
    ov[255, :] = 0.0
    ov = np.concatenate([ov, np.ones((256, 1), np.float32)], axis=1)
    c["c_ov"] = ov.reshape(2, 128, 65).transpose(1, 0, 2).copy()
    tq = np.arange(T)
    cur = tq // 64
    forced = (j[None, :] == 0) | (j[None, :] == cur[:, None]) | (j[None, :] == cur[:, None] - 1)
    fb = np.where(j[None, :] <= cur[:, None], 1000.0 * forced, -1e9).astype(np.float32)
    c["c_fb"] = fb.reshape(32, 128, 64).transpose(1, 0, 2).copy()
    import ml_dtypes
    qoff = (np.arange(T) % NT).astype(np.float32)
    c["c_cq"] = (np.asarray(SLOPES, np.float32)[:, None] * -qoff[None, :]).astype(np.float32).astype(ml_dtypes.bfloat16)
    c["c_ident"] = np.eye(128, dtype=np.float32)
    c["c_identbig"] = (np.eye(128) * BIG).astype(np.float32)
    return c


CONST_SHAPES = {
    "c_mask": [128, 13, 512], "c_E": [64, T], "c_bias": [128, 612], "c_iota": [128, 512],
    "c_ov": [128, 2, 65], "c_fb": [128, 32, 64], "c_ident": [128, 128], "c_identbig": [128, 128],
}


def host_small(inp):
    sm = np.zeros((128, DEPTH, NSM), np.float32)
    for l in range(DEPTH):
        sm[:, l, 0:8] = np.asarray(inp["attn_norm"][l]).reshape(8, 128).T
        sm[:, l, 8:16] = np.asarray(inp["ffn_norm"][l]).reshape(8, 128).T
        cw = np.asarray(inp["conv_w"][l])
        for c in range(2):
            sm[:, l, 16 + 31 * c: 16 + 31 * (c + 1)] = cw[:, 128 * c: 128 * (c + 1)].T
        sm[:, l, 78:80] = np.asarray(inp["conv_b"][l]).reshape(2, 128).T
        sm[:, l, 80:82] = np.asarray(inp["conv_ln_g"][l]).reshape(2, 128).T
        sm[:, l, 82:84] = np.asarray(inp["conv_ln_b"][l]).reshape(2, 128).T
    fin = np.asarray(inp["final_norm"]).reshape(8, 128).T.copy()
    pe = np.zeros((128, DEPTH, 2, 32), np.float32)
    for l in range(DEPTH):
        for kv, nm in enumerate(("cmp_k_pe", "cmp_v_pe")):
            p = np.asarray(inp[nm][l]).T
            pe[0:64, l, kv] = p
            pe[64:128, l, kv] = p
    return sm, fin, pe


class Prog:
    def __init__(self, nc, es, debug=(), ext_in=()):
        self.nc = nc
        self.kb = KB(nc, es)
        self.debug = set(debug)
        self.ext_in = set(ext_in)
        nc_ = nc

        def inp(name, shape, dt=F32):
            return nc_.dram_tensor(name, list(shape), dt, kind="ExternalInput").ap()

        self.xin = inp("xT", [D, T])
        self.w_in = inp("w_in", [DEPTH, D, INC])
        self.w1k = inp("cmp_k_w1", [DEPTH, 32, 64, 256])
        self.w2k = inp("cmp_k_w2", [DEPTH, 256, 64])
        self.w1v = inp("cmp_v_w1", [DEPTH, 32, 64, 256])
        self.w2v = inp("cmp_v_w2", [DEPTH, 256, 64])
        self.w_out = inp("w_out", [DEPTH, D, D])
        self.w_gu = inp("w_gate_up", [DEPTH, D, 2 * DFF])
        self.w_dn = inp("w_down", [DEPTH, DFF, D])
        self.small = inp("c_small", [128, DEPTH, NSM])
        self.fin = inp("c_fin", [128, 8])
        self.pe_t = inp("c_pe", [128, DEPTH, 2, 32])
        self.cst = {k: inp(k, v) for k, v in CONST_SHAPES.items()}
        self.cst["c_cq"] = inp("c_cq", [NQ, T], BF16)

        self.outT = nc.dram_tensor("outT", [D, T], F32, kind="ExternalOutput").ap()
        self.xres = self.scr("xres", [D, T], F32)
        self.YT = self.scr("YT", [CC, T], F32)
        self.QT = self.scr("QT", [768, T], BF16)
        self.KCRT = self.scr("KCRT", [256, T], BF16)
        self.VCRT = self.scr("VCRT", [256, T], BF16)
        self.KST = self.scr("KST", [256, T], BF16)
        self.KWT = self.scr("KWT", [256, T], BF16)
        self.VSd = self.scr("VSd", [T, 256], BF16)
        self.VWd = self.scr("VWd", [T, 256], BF16)
        self.GT = self.scr("GT", [36, T], F32)
        self.MIXT = self.scr("MIXT", [D, T], BF16)
        if "KCA" in self.debug:
            self.dbg_kca = nc.dram_tensor("dbg_kca", [64, NKV, 256], BF16, kind="ExternalOutput").ap()
            self.dbg_vca = nc.dram_tensor("dbg_vca", [128, NKV, 2, 64], BF16, kind="ExternalOutput").ap()
        self.b_x = [Buf() for _ in range(NTT)]
        self.b_a = [Buf() for _ in range(NTT)]
        self.b_mix = [Buf() for _ in range(NTT)]

    def scr(self, name, shape, dt):
        kind = "ExternalOutput" if name in self.debug else ("ExternalInput" if name in self.ext_in else "Internal")
        return self.nc.dram_tensor(name, list(shape), dt, kind=kind).ap()

    def phase_a(self, l):
        kb, nc = self.kb, self.nc
        pe, act, dve, pool, sp = kb.pe, kb.act, kb.dve, kb.pool, kb.sp
        xsrc = self.xin if l == 0 else self.xres
        xv = xsrc.rearrange("(kc p) t -> p kc t", p=128)
        with Ph(kb) as ph:
            WIN = ph.sb("a_win", [128, KC, INC], BF16)
            SM = ph.sb("a_sm", [128, NSM])
            ONESB = ph.sb("a_ones", [128, 128], BF16)
            EPSC = ph.sb("a_epsc", [128, 1])
            XT = [ph.sb(f"a_xt{i}", [128, KC, NT]) for i in range(2)]
            SQ = [ph.sb(f"a_sq{i}", [128, KC, NT], BF16) for i in range(2)]
            XS = [ph.sb(f"a_xs{i}", [128, KC, NT], BF16) for i in range(2)]
            RS = [ph.sb(f"a_rs{i}", [128, NT]) for i in range(2)]
            EV = [ph.sb(f"a_ev{i}", [128, NT], BF16) for i in range(4)]
            EF = [ph.sb(f"a_ef{i}", [128, NT]) for i in range(2)]
            SG = [ph.sb(f"a_sg{i}", [128, NT]) for i in range(2)]
            EVV = [ph.sb(f"a_evv{i}", [128, 512], BF16) for i in range(2)]
            PS = [ph.ps(f"a_ps{i}", [128, NT]) for i in range(6)]
            PSS = ph.ps("a_pss", [128, NT])
            YPT = [[ph.sb(f"a_yp{c}_{k}", [128, CW - 1 + NT]) for k in range(2)] for c in range(2)]
            CACC = [ph.sb(f"a_cacc{c}", [128, NT]) for c in range(2)]
            CAB = [ph.sb(f"a_cab{c}", [128, NT], BF16) for c in range(2)]
            CSQ = [ph.sb(f"a_csq{c}", [128, NT], BF16) for c in range(2)]
            CMEAN = ph.sb("a_cmean", [128, NT])
            CVAR = ph.sb("a_cvar", [128, NT])
            CDT = [ph.sb(f"a_cdt{c}", [128, NT]) for c in range(2)]
            COB = [ph.sb(f"a_cob{c}", [128, NT], BF16) for c in range(2)]
            bYPT = [[Buf(), Buf()], [Buf(), Buf()]]
            bCACC = [Buf(), Buf()]; bCAB = [Buf(), Buf()]; bCSQ = [Buf(), Buf()]
            bCMEAN, bCVAR = Buf(), Buf()
            bCDT = [Buf(), Buf()]; bCOB = [Buf(), Buf()]
            bWIN, bSM, bONES, bPSS = Buf(), Buf(), Buf(), Buf()
            bXT = [Buf(), Buf()]; bSQ = [Buf(), Buf()]; bXS = [Buf(), Buf()]; bRS = [Buf(), Buf()]
            bEV = [Buf() for _ in EV]; bEF = [Buf() for _ in EF]; bSG = [Buf() for _ in SG]
            bEVV = [Buf() for _ in EVV]; bPS = [Buf() for _ in PS]

            kb.dma(sp, SM[:], self.small[:, l, :], writes=[bSM])
            kb.op(dve, lambda: nc.vector.memset(ONESB[:], 1.0), writes=[bONES])
            kb.op(dve, lambda: nc.vector.memset(EPSC[:], EPS), writes=[bONES])
            for c in range(2):
                kb.op(dve, lambda c=c: nc.vector.memset(YPT[c][0][:, 0:CW - 1], 0.0), writes=[bYPT[c][0]])
            wv = self.w_in[l].rearrange("(kc p) n -> p kc n", p=128)
            wcols = [(0, 512), (512, 1280), (1280, 2048), (2048, INC)]
            bWINc = [Buf() for _ in wcols]
            for (c0_, c1_), bw in zip(wcols, bWINc):
                for k0 in range(0, KC, 4):
                    kb.dma(pool, WIN[:, k0:k0 + 4, c0_:c1_], wv[:, k0:k0 + 4, c0_:c1_], writes=[bw])

            def bwin(col):
                for (c0_, c1_), bw in zip(wcols, bWINc):
                    if c0_ <= col < c1_:
                        return bw
                raise AssertionError(col)

            def prep_load(i):
                j = i % 2
                kb.dma(sp, XT[j][:], xv[:, :, i * NT:(i + 1) * NT], reads=[self.b_x[i]], writes=[bXT[j]])

            def prep(i):
                j = i % 2
                kb.op(act, lambda: nc.scalar.activation(out=SQ[j][:], in_=XT[j][:], func=AF.Square),
                      reads=[bXT[j]], writes=[bSQ[j]])
                for kc in range(KC):
                    kb.op(pe, lambda kc=kc: nc.tensor.matmul(PSS[:], lhsT=ONESB[:], rhs=SQ[j][:, kc, :],
                                                             start=(kc == 0), stop=(kc == KC - 1)),
                          reads=[bONES, bSQ[j]], writes=[bPSS])
                kb.op(dve, lambda: nc.vector.tensor_scalar(out=RS[j][:], in0=PSS[:], scalar1=1.0 / D, scalar2=EPS,
                                                           op0=ALU.mult, op1=ALU.add), reads=[bPSS], writes=[bRS[j]])
                kb.op(act, lambda: nc.scalar.activation(out=RS[j][:], in_=RS[j][:], func=AF.Sqrt),
                      reads=[bRS[j]], writes=[bRS[j]])
                kb.op(dve, lambda: nc.vector.reciprocal(out=RS[j][:], in_=RS[j][:]), reads=[bRS[j]], writes=[bRS[j]])
                for kc in range(KC):
                    kb.op(dve, lambda kc=kc: nc.vector.scalar_tensor_tensor(
                        out=XS[j][:, kc, :], in0=XT[j][:, kc, :], scalar=SM[:, kc:kc + 1], in1=RS[j][:],
                        op0=ALU.mult, op1=ALU.mult), reads=[bXT[j], bSM, bRS[j]], writes=[bXS[j]])

            cnt = {"ps": 0, "ev": 0, "ef": 0, "sg": 0, "evv": 0}

            def nxt(key, n):
                v = cnt[key] % n
                cnt[key] += 1
                return v

            def mm_chunk(j, c, w=128):
                p = nxt("ps", len(PS))
                for kc in range(KC):
                    kb.op(pe, lambda kc=kc: nc.tensor.matmul(PS[p][0:w, :], lhsT=WIN[:, kc, 128 * c:128 * c + w],
                                                             rhs=XS[j][:, kc, :], start=(kc == 0), stop=(kc == KC - 1)),
                          reads=[bwin(128 * c), bXS[j]], writes=[bPS[p]])
                return p

            def store_bf(i, p, dst, scale=None, eng_sel=0):
                e = nxt("ev", len(EV))
                if eng_sel == 0:
                    if scale is None:
                        kb.op(act, lambda: nc.scalar.copy(out=EV[e][:], in_=PS[p][:]), reads=[bPS[p]], writes=[bEV[e]])
                    else:
                        kb.op(act, lambda: nc.scalar.mul(out=EV[e][:], in_=PS[p][:], mul=scale),
                              reads=[bPS[p]], writes=[bEV[e]])
                else:
                    if scale is None:
                        kb.op(dve, lambda: nc.vector.tensor_copy(out=EV[e][:], in_=PS[p][:]),
                              reads=[bPS[p]], writes=[bEV[e]])
                    else:
                        kb.op(dve, lambda: nc.vector.tensor_scalar(out=EV[e][:], in0=PS[p][:], scalar1=scale,
                                                                   scalar2=None, op0=ALU.mult),
                              reads=[bPS[p]], writes=[bEV[e]])
                kb.dma(sp, dst[:, i * NT:(i + 1) * NT], EV[e][:], reads=[bEV[e]], writes=[self.b_a[i]])

            def main(i, mid=None):
                j = i % 2
                k2 = i % 2
                sl = slice(i * NT, (i + 1) * NT)
                for c in range(2):
                    pu = mm_chunk(j, c)
                    pv = mm_chunk(j, c + 2)
                    s_ = nxt("sg", 2)
                    kb.op(act, lambda s_=s_, pv=pv: nc.scalar.activation(out=SG[s_][:], in_=PS[pv][:], func=AF.Sigmoid),
                          reads=[bPS[pv]], writes=[bSG[s_]])
                    if i > 0:
                        kb.op(act, lambda c=c: nc.scalar.copy(out=YPT[c][k2][:, 0:CW - 1],
                                                              in_=YPT[c][1 - k2][:, NT:NT + CW - 1]),
                              reads=[bYPT[c][1 - k2]], writes=[bYPT[c][k2]])
                    kb.op(dve, lambda c=c, pu=pu, s_=s_: nc.vector.tensor_tensor(out=YPT[c][k2][:, CW - 1:], in0=PS[pu][:],
                                                                                 in1=SG[s_][:], op=ALU.mult),
                          reads=[bPS[pu], bSG[s_]], writes=[bYPT[c][k2]])
                for tj in range(CW):
                    for c in range(2):
                        w0 = 16 + 31 * c
                        if tj == 0:
                            kb.op(dve, lambda c=c, w0=w0: nc.vector.tensor_scalar(
                                out=CACC[c][:], in0=YPT[c][k2][:, 0:NT], scalar1=SM[:, w0:w0 + 1],
                                scalar2=SM[:, 78 + c:79 + c], op0=ALU.mult, op1=ALU.add),
                                reads=[bYPT[c][k2], bSM], writes=[bCACC[c]])
                        else:
                            kb.op(dve, lambda c=c, tj=tj, w0=w0: nc.vector.scalar_tensor_tensor(
                                out=CACC[c][:], in0=YPT[c][k2][:, tj:tj + NT], scalar=SM[:, w0 + tj:w0 + tj + 1],
                                in1=CACC[c][:], op0=ALU.mult, op1=ALU.add),
                                reads=[bYPT[c][k2], bSM, bCACC[c]], writes=[bCACC[c]])
                for c in range(4, 10):
                    p = mm_chunk(j, c)
                    store_bf(i, p, self.QT[128 * (c - 4):128 * (c - 3), :], scale=0.125, eng_sel=0)
                if mid is not None:
                    mid()
                for c, dst in ((10, self.KCRT), (11, self.KCRT), (12, self.VCRT), (13, self.VCRT),
                               (14, self.KST), (15, self.KST), (18, self.KWT), (19, self.KWT)):
                    p = mm_chunk(j, c)
                    r0 = 128 * (c % 2)
                    store_bf(i, p, dst[r0:r0 + 128, :], eng_sel=0)
                p = mm_chunk(j, 22, w=36)
                f = nxt("ef", 2)
                kb.op(act, lambda: nc.scalar.activation(out=EF[f][0:36, :], in_=PS[p][0:36, :], func=AF.Sigmoid),
                      reads=[bPS[p]], writes=[bEF[f]])
                kb.dma(sp, self.GT[:, i * NT:(i + 1) * NT], EF[f][0:36, :], reads=[bEF[f]], writes=[self.b_a[i]])
                for s4 in range(4):
                    p = nxt("ps", len(PS))
                    for half, c0 in ((0, 2048), (1, 2560)):
                        for kc in range(KC):
                            kb.op(pe, lambda kc=kc, half=half, c0=c0: nc.tensor.matmul(
                                PS[p][:, 256 * half:256 * (half + 1)], lhsT=XS[j][:, kc, 128 * s4:128 * (s4 + 1)],
                                rhs=WIN[:, kc, c0:c0 + 256], start=(kc == 0), stop=(kc == KC - 1)),
                                reads=[bwin(c0), bXS[j]], writes=[bPS[p]])
                    e = nxt("evv", 2)
                    kb.op(act, lambda: nc.scalar.copy(out=EVV[e][:], in_=PS[p][:]), reads=[bPS[p]], writes=[bEVV[e]])
                    t0 = i * NT + 128 * s4
                    kb.dma(sp, self.VSd[t0:t0 + 128, :], EVV[e][:, 0:256], reads=[bEVV[e]], writes=[self.b_a[i]])
                    kb.dma(sp, self.VWd[t0:t0 + 128, :], EVV[e][:, 256:512], reads=[bEVV[e]], writes=[self.b_a[i]])
                for c in range(2):
                    kb.op(act, lambda c=c: nc.scalar.copy(out=CAB[c][:], in_=CACC[c][:]), reads=[bCACC[c]], writes=[bCAB[c]])
                    kb.op(act, lambda c=c: nc.scalar.activation(out=CSQ[c][:], in_=CACC[c][:], func=AF.Square),
                          reads=[bCACC[c]], writes=[bCSQ[c]])
                pm = nxt("ps", len(PS))
                for c in range(2):
                    kb.op(pe, lambda c=c: nc.tensor.matmul(PS[pm][:], lhsT=ONESB[:], rhs=CAB[c][:], start=(c == 0), stop=(c == 1)),
                          reads=[bONES, bCAB[c]], writes=[bPS[pm]])
                pv2 = nxt("ps", len(PS))
                for c in range(2):
                    kb.op(pe, lambda c=c: nc.tensor.matmul(PS[pv2][:], lhsT=ONESB[:], rhs=CSQ[c][:], start=(c == 0), stop=(c == 1)),
                          reads=[bONES, bCSQ[c]], writes=[bPS[pv2]])
                kb.op(dve, lambda: nc.vector.tensor_scalar(out=CMEAN[:], in0=PS[pm][:], scalar1=1.0 / CC, scalar2=None,
                                                           op0=ALU.mult), reads=[bPS[pm]], writes=[bCMEAN])
                kb.op(pool, lambda: nc.gpsimd.tensor_tensor(out=CVAR[:], in0=CMEAN[:], in1=CMEAN[:], op=ALU.mult),
                      reads=[bCMEAN], writes=[bCVAR])
                kb.op(dve, lambda: nc.vector.scalar_tensor_tensor(out=CVAR[:], in0=PS[pv2][:], scalar=1.0 / CC, in1=CVAR[:],
                                                                  op0=ALU.mult, op1=ALU.subtract),
                      reads=[bPS[pv2], bCVAR], writes=[bCVAR])
                kb.op(act, lambda: nc.scalar.activation(out=CVAR[:], in_=CVAR[:], func=AF.Sqrt, bias=EPSC[:, 0:1]),
                      reads=[bCVAR, bONES], writes=[bCVAR])
                kb.op(dve, lambda: nc.vector.reciprocal(out=CVAR[:], in_=CVAR[:]), reads=[bCVAR], writes=[bCVAR])
                for c in range(2):
                    kb.op(pool, lambda c=c: nc.gpsimd.tensor_tensor(out=CDT[c][:], in0=CACC[c][:], in1=CMEAN[:],
                                                                    op=ALU.subtract),
                          reads=[bCACC[c], bCMEAN], writes=[bCDT[c]])
                    kb.op(pool, lambda c=c: nc.gpsimd.tensor_tensor(out=CDT[c][:], in0=CDT[c][:], in1=CVAR[:], op=ALU.mult),
                          reads=[bCDT[c], bCVAR], writes=[bCDT[c]])
                    kb.op(act, lambda c=c: nc.scalar.activation(out=COB[c][:], in_=CDT[c][:], func=AF.Silu,
                                                                bias=SM[:, 82 + c:83 + c], scale=SM[:, 80 + c:81 + c]),
                          reads=[bCDT[c], bSM], writes=[bCOB[c]])
                    kb.dma(sp, self.MIXT[128 * c:128 * (c + 1), sl], COB[c][:], reads=[bCOB[c]], writes=[self.b_mix[i]])

            prep_load(0)
            prep(0)
            prep_load(1)
            for i in range(NTT):
                def mid(i=i):
                    prep(i + 1)
                    if i + 2 < NTT:
                        prep_load(i + 2)
                main(i, mid=(mid if i + 1 < NTT else None))


    def phase_c(self, l, last):
        kb, nc = self.kb, self.nc
        with Ph(kb) as pho:
            WGU = pho.sb("c2_wgu", [128, KC, 2 * DFF], BF16)
            WD = pho.sb("c2_wd", [128, FC, D], BF16)
            wts = (WGU, WD, Buf(), Buf())
            self.phase_c1(l, wts)
            self.phase_c2(l, last, wts)

    def phase_c1(self, l, wts):
        kb, nc = self.kb, self.nc
        pe, act, dve, pool, sp = kb.pe, kb.act, kb.dve, kb.pool, kb.sp
        xsrc = self.xin if l == 0 else self.xres
        xv = xsrc.rearrange("(kc p) t -> p kc t", p=128)
        xo = self.xres.rearrange("(kc p) t -> p kc t", p=128)
        mv = self.MIXT.rearrange("(kc p) t -> p kc t", p=128)
        with Ph(kb) as ph:
            WO = ph.sb("c1_wo", [128, KC, D], BF16)
            XT = [ph.sb(f"c1_xt{i}", [128, KC, NT]) for i in range(2)]
            MX = [ph.sb(f"c1_mx{i}", [128, KC, NT], BF16) for i in range(2)]
            PS = [ph.ps(f"c1_ps{i}", [128, NT]) for i in range(4)]
            bWO = Buf(); bXT = [Buf(), Buf()]; bMX = [Buf(), Buf()]; bPS = [Buf() for _ in PS]
            wv = self.w_out[l].rearrange("(kc p) n -> p kc n", p=128)
            for kc in range(KC):
                kb.dma(pool, WO[:, kc, :], wv[:, kc, :], writes=[bWO])
            WGU, WD, bW, bWD = wts
            gv = self.w_gu[l].rearrange("(kc p) n -> p kc n", p=128)
            for kc in range(KC):
                for hf in range(2):
                    kb.dma(pool, WGU[:, kc, DFF * hf:DFF * (hf + 1)], gv[:, kc, DFF * hf:DFF * (hf + 1)], writes=[bW])
            dv = self.w_dn[l].rearrange("(f p) n -> p f n", p=128)
            for f0 in range(0, FC, 6):
                f1 = min(FC, f0 + 6)
                kb.dma(pool, WD[:, f0:f1, :], dv[:, f0:f1, :], writes=[bWD])

            def load(i):
                j = i % 2
                kb.dma(sp, XT[j][:], xv[:, :, i * NT:(i + 1) * NT], reads=[self.b_x[i]], writes=[bXT[j]])
                kb.dma(sp, MX[j][:], mv[:, :, i * NT:(i + 1) * NT], reads=[self.b_mix[i]], writes=[bMX[j]])

            load(0)
            pc = 0
            for i in range(NTT):
                j = i % 2
                if i + 1 < NTT:
                    load(i + 1)
                for n in range(KC):
                    p = pc % len(PS); pc += 1
                    for kc in range(KC):
                        kb.op(pe, lambda kc=kc, n=n, p=p: nc.tensor.matmul(
                            PS[p][:], lhsT=WO[:, kc, 128 * n:128 * (n + 1)], rhs=MX[j][:, kc, :],
                            start=(kc == 0), stop=(kc == KC - 1)), reads=[bWO, bMX[j]], writes=[bPS[p]])
                    kb.op(dve, lambda n=n, p=p: nc.vector.tensor_tensor(out=XT[j][:, n, :], in0=XT[j][:, n, :],
                                                                          in1=PS[p][:], op=ALU.add),
                          reads=[bPS[p], bXT[j]], writes=[bXT[j]])
                kb.dma(sp, xo[:, :, i * NT:(i + 1) * NT], XT[j][:], reads=[bXT[j]], writes=[self.b_x[i]])

    def phase_c2(self, l, last, wts):
        kb, nc = self.kb, self.nc
        WGU, WD, bW, bWD = wts
        pe, act, dve, pool, sp = kb.pe, kb.act, kb.dve, kb.pool, kb.sp
        xv = self.xres.rearrange("(kc p) t -> p kc t", p=128)
        ov = self.outT.rearrange("(kc p) t -> p kc t", p=128)
        with Ph(kb) as ph:
            SM = ph.sb("c2_sm", [128, NSM])
            FIN = ph.sb("c2_fin", [128, 8])
            ONESB = ph.sb("c2_ones", [128, 128], BF16)
            XT = [ph.sb(f"c2_xt{i}", [128, KC, NT]) for i in range(2)]
            XS = [ph.sb(f"c2_xs{i}", [128, KC, NT], BF16) for i in range(2)]
            RS0 = ph.sb("c2_rs", [128, NT])
            RS = [RS0, RS0]
            ACTT = ph.sb("c2_actt", [128, FC, NT], BF16)
            SQ = ph.sb("c2_sq", [128, 1, NT], BF16)
            bSQ = Buf()
            SG0 = ph.sb("c2_sg", [128, NT])
            SG = [SG0, SG0]
            PS = [ph.ps(f"c2_ps{i}", [128, NT]) for i in range(7)]
            PSS = ph.ps("c2_pss", [128, NT])
            bSM, bONES, bPSS, bACTT = Buf(), Buf(), Buf(), Buf()
            bXT = [Buf(), Buf()]; bXS = [Buf(), Buf()]
            bRS0 = Buf(); bRS = [bRS0, bRS0]; bSG0 = Buf(); bSG = [bSG0, bSG0]
            bPS = [Buf() for _ in PS]
            kb.dma(sp, SM[:], self.small[:, l, :], writes=[bSM])
            kb.dma(sp, FIN[:], self.fin[:, :], writes=[bSM])
            kb.op(dve, lambda: nc.vector.memset(ONESB[:], 1.0), writes=[bONES])
            def stats(j, gcol, gt):
                for kc in range(KC):
                    kb.op(act, lambda kc=kc: nc.scalar.activation(out=SQ[:, 0, :], in_=XT[j][:, kc, :], func=AF.Square),
                          reads=[bXT[j]], writes=[bSQ])
                    kb.op(pe, lambda kc=kc: nc.tensor.matmul(PSS[:], lhsT=ONESB[:], rhs=SQ[:, 0, :],
                                                             start=(kc == 0), stop=(kc == KC - 1)),
                          reads=[bONES, bSQ], writes=[bPSS])
                kb.op(dve, lambda: nc.vector.tensor_scalar(out=RS[j][:], in0=PSS[:], scalar1=1.0 / D, scalar2=EPS,
                                                           op0=ALU.mult, op1=ALU.add), reads=[bPSS], writes=[bRS[j]])
                kb.op(act, lambda: nc.scalar.activation(out=RS[j][:], in_=RS[j][:], func=AF.Sqrt),
                      reads=[bRS[j]], writes=[bRS[j]])
                kb.op(dve, lambda: nc.vector.reciprocal(out=RS[j][:], in_=RS[j][:]), reads=[bRS[j]], writes=[bRS[j]])

            def prep_load(i):
                j = i % 2
                kb.dma(sp, XT[j][:], xv[:, :, i * NT:(i + 1) * NT], reads=[self.b_x[i]], writes=[bXT[j]])

            def prep(i):
                j = i % 2
                stats(j, 8, SM)
                for kc in range(KC):
                    kb.op(dve, lambda kc=kc: nc.vector.scalar_tensor_tensor(
                        out=XS[j][:, kc, :], in0=XT[j][:, kc, :], scalar=SM[:, 8 + kc:9 + kc], in1=RS[j][:],
                        op0=ALU.mult, op1=ALU.mult), reads=[bXT[j], bSM, bRS[j]], writes=[bXS[j]])

            pc = [0]

            def nps():
                p = pc[0] % len(PS)
                pc[0] += 1
                return p

            def main(i, mid=None):
                j = i % 2
                if i + 1 < NTT:
                    prep_load(i + 1)
                for f in range(FC):
                    if f == 8 and mid is not None:
                        mid()
                    pg_ = nps()
                    for kc in range(KC):
                        kb.op(pe, lambda kc=kc, f=f, p=pg_: nc.tensor.matmul(
                            PS[p][:], lhsT=WGU[:, kc, 128 * f:128 * (f + 1)], rhs=XS[j][:, kc, :],
                            start=(kc == 0), stop=(kc == KC - 1)), reads=[bW, bXS[j]], writes=[bPS[pg_]])
                    pu_ = nps()
                    for kc in range(KC):
                        kb.op(pe, lambda kc=kc, f=f, p=pu_: nc.tensor.matmul(
                            PS[p][:], lhsT=WGU[:, kc, DFF + 128 * f:DFF + 128 * (f + 1)], rhs=XS[j][:, kc, :],
                            start=(kc == 0), stop=(kc == KC - 1)), reads=[bW, bXS[j]], writes=[bPS[pu_]])
                    s = f % 2
                    kb.op(act, lambda s=s, p=pg_: nc.scalar.activation(out=SG[s][:], in_=PS[p][:], func=AF.Silu),
                          reads=[bPS[pg_]], writes=[bSG[s]])
                    kb.op(dve, lambda s=s, p=pu_, f=f: nc.vector.tensor_tensor(out=ACTT[:, f, :], in0=PS[p][:],
                                                                               in1=SG[s][:], op=ALU.mult),
                          reads=[bPS[pu_], bSG[s]], writes=[bACTT])
                for n in range(KC):
                    p = nps()
                    for f in range(FC):
                        kb.op(pe, lambda f=f, n=n, p=p: nc.tensor.matmul(
                            PS[p][:], lhsT=WD[:, f, 128 * n:128 * (n + 1)], rhs=ACTT[:, f, :],
                            start=(f == 0), stop=(f == FC - 1)), reads=[bWD, bACTT], writes=[bPS[p]])
                    kb.op(dve, lambda n=n, p=p: nc.vector.tensor_tensor(out=XT[j][:, n, :], in0=XT[j][:, n, :],
                                                                          in1=PS[p][:], op=ALU.add),
                          reads=[bPS[p], bXT[j]], writes=[bXT[j]])
                if not last:
                    kb.dma(sp, xv[:, :, i * NT:(i + 1) * NT], XT[j][:], reads=[bXT[j]], writes=[self.b_x[i]])
                else:
                    stats(j, 0, FIN)
                    for kc in range(KC):
                        kb.op(dve, lambda kc=kc: nc.vector.scalar_tensor_tensor(
                            out=XT[j][:, kc, :], in0=XT[j][:, kc, :], scalar=FIN[:, kc:kc + 1], in1=RS[j][:],
                            op0=ALU.mult, op1=ALU.mult), reads=[bXT[j], bSM, bRS[j]], writes=[bXT[j]])
                    kb.dma(sp, ov[:, :, i * NT:(i + 1) * NT], XT[j][:], reads=[bXT[j]], writes=[self.b_x[i]])

            prep_load(0)
            prep(0)
            for i in range(NTT):
                main(i, mid=((lambda i=i: prep(i + 1)) if i + 1 < NTT else None))


    def phase_b0(self, l):
        kb, nc = self.kb, self.nc
        pe, act, dve, pool, sp = kb.pe, kb.act, kb.dve, kb.pool, kb.sp
        with Ph(kb) as ph:
            SM = ph.sb("b0_sm", [128, NSM])
            ONESB = ph.sb("b0_ones", [128, 128], BF16)
            YP = ph.sb("b0_yp", [128, 2, CW - 1 + T])
            ACC = ph.sb("b0_acc", [128, 2, T])
            AB = ph.sb("b0_ab", [128, 2, T], BF16)
            SQB = ph.sb("b0_sqb", [128, 2, T], BF16)
            MEAN = [ph.sb(f"b0_mean{i}", [128, NT]) for i in range(2)]
            VAR = [ph.sb(f"b0_var{i}", [128, NT]) for i in range(2)]
            DT = [ph.sb(f"b0_dt{i}", [128, NT]) for i in range(2)]
            OB = [ph.sb(f"b0_ob{i}", [128, NT], BF16) for i in range(2)]
            PM = [ph.ps(f"b0_pm{i}", [128, NT]) for i in range(2)]
            PV = [ph.ps(f"b0_pv{i}", [128, NT]) for i in range(2)]
            bSM, bONES = Buf(), Buf()
            bYP = [Buf(), Buf()]; bACC = [Buf(), Buf()]; bAB = [Buf(), Buf()]; bSQB = [Buf(), Buf()]
            bMEAN = [Buf(), Buf()]; bVAR = [Buf(), Buf()]; bDT = [Buf(), Buf()]; bOB = [Buf(), Buf()]
            bPM = [Buf(), Buf()]; bPV = [Buf(), Buf()]
            kb.dma(sp, SM[:], self.small[:, l, :], writes=[bSM])
            kb.op(dve, lambda: nc.vector.memset(ONESB[:], 1.0), writes=[bONES])
            for c in range(2):
                kb.op(dve, lambda c=c: nc.vector.memset(YP[:, c, 0:CW - 1], 0.0), writes=[bYP[c]])
                kb.dma(sp, YP[:, c, CW - 1:], self.YT[128 * c:128 * (c + 1), :], writes=[bYP[c]])
            for j in range(CW):
                for c in range(2):
                    w0 = 16 + 31 * c
                    eng, eh = dve, nc.vector
                    if j == 0:
                        kb.op(eng, lambda c=c, w0=w0, eh=eh: eh.tensor_scalar(
                            out=ACC[:, c, :], in0=YP[:, c, 0:T], scalar1=SM[:, w0:w0 + 1], scalar2=SM[:, 78 + c:79 + c],
                            op0=ALU.mult, op1=ALU.add), reads=[bYP[c], bSM], writes=[bACC[c]])
                    else:
                        kb.op(eng, lambda c=c, j=j, w0=w0, eh=eh: eh.scalar_tensor_tensor(
                            out=ACC[:, c, :], in0=YP[:, c, j:j + T], scalar=SM[:, w0 + j:w0 + j + 1], in1=ACC[:, c, :],
                            op0=ALU.mult, op1=ALU.add), reads=[bYP[c], bSM, bACC[c]], writes=[bACC[c]])
            for c in range(2):
                kb.op(act, lambda c=c: nc.scalar.copy(out=AB[:, c, :], in_=ACC[:, c, :]), reads=[bACC[c]], writes=[bAB[c]])
                kb.op(act, lambda c=c: nc.scalar.activation(out=SQB[:, c, :], in_=ACC[:, c, :], func=AF.Square),
                      reads=[bACC[c]], writes=[bSQB[c]])
            for i in range(NTT):
                j = i % 2
                sl = slice(i * NT, (i + 1) * NT)
                for c in range(2):
                    kb.op(pe, lambda c=c: nc.tensor.matmul(PM[j][:], lhsT=ONESB[:], rhs=AB[:, c, sl],
                                                           start=(c == 0), stop=(c == 1)),
                          reads=[bONES, bAB[c]], writes=[bPM[j]])
                for c in range(2):
                    kb.op(pe, lambda c=c: nc.tensor.matmul(PV[j][:], lhsT=ONESB[:], rhs=SQB[:, c, sl],
                                                           start=(c == 0), stop=(c == 1)),
                          reads=[bONES, bSQB[c]], writes=[bPV[j]])
                kb.op(dve, lambda: nc.vector.tensor_scalar(out=MEAN[j][:], in0=PM[j][:], scalar1=1.0 / CC, scalar2=None,
                                                           op0=ALU.mult), reads=[bPM[j]], writes=[bMEAN[j]])
                kb.op(dve, lambda: nc.vector.tensor_tensor(out=VAR[j][:], in0=MEAN[j][:], in1=MEAN[j][:], op=ALU.mult),
                      reads=[bMEAN[j]], writes=[bVAR[j]])
                kb.op(dve, lambda: nc.vector.scalar_tensor_tensor(out=VAR[j][:], in0=PV[j][:], scalar=1.0 / CC,
                                                                  in1=VAR[j][:], op0=ALU.mult, op1=ALU.subtract),
                      reads=[bPV[j], bVAR[j]], writes=[bVAR[j]])
                kb.op(dve, lambda: nc.vector.tensor_scalar(out=VAR[j][:], in0=VAR[j][:], scalar1=EPS, scalar2=None,
                                                           op0=ALU.add), reads=[bVAR[j]], writes=[bVAR[j]])
                kb.op(act, lambda: nc.scalar.activation(out=VAR[j][:], in_=VAR[j][:], func=AF.Sqrt),
                      reads=[bVAR[j]], writes=[bVAR[j]])
                kb.op(dve, lambda: nc.vector.reciprocal(out=VAR[j][:], in_=VAR[j][:]), reads=[bVAR[j]], writes=[bVAR[j]])
                for c in range(2):
                    kb.op(dve, lambda c=c: nc.vector.tensor_tensor(out=DT[c][:], in0=ACC[:, c, sl], in1=MEAN[j][:],
                                                                   op=ALU.subtract),
                          reads=[bACC[c], bMEAN[j]], writes=[bDT[c]])
                    kb.op(dve, lambda c=c: nc.vector.tensor_tensor(out=DT[c][:], in0=DT[c][:], in1=VAR[j][:], op=ALU.mult),
                          reads=[bDT[c], bVAR[j]], writes=[bDT[c]])
                    kb.op(act, lambda c=c: nc.scalar.activation(out=OB[c][:], in_=DT[c][:], func=AF.Silu,
                                                                bias=SM[:, 82 + c:83 + c], scale=SM[:, 80 + c:81 + c]),
                          reads=[bDT[c], bSM], writes=[bOB[c]])
                    kb.dma(pool, self.MIXT[128 * c:128 * (c + 1), sl], OB[c][:], reads=[bOB[c]], writes=[self.b_mix[i]])

    def phase_b12(self, l, groups=range(NKV), qtiles=range(NTT)):
        kb, nc = self.kb, self.nc
        pe, act, dve, pool, sp = kb.pe, kb.act, kb.dve, kb.pool, kb.sp
        with Ph(kb) as ph:
            MASK = ph.sb("b_mask", [128, 13, NT], BF16)
            BIAS = ph.sb("b_bias", [128, 612])
            IOTA = ph.sb("b_iota", [128, NT])
            OV = ph.sb("b_ov", [128, 2, 65], BF16)
            FB = ph.sb("b_fb", [128, 32, 64])
            IDENT = ph.sb("b_ident", [128, 128], BF16)
            IDENTBIG = ph.sb("b_identbig", [128, 128], BF16)
            ONESF = ph.sb("b_onesf", [128, 64])
            KCA = ph.sb("b_kca", [128, NKV, 256], BF16)
            VCA = ph.sb("b_vca", [128, NKV, 2, 128], BF16)
            PT = [ph.sb(f"b_pt{i}", [128, NT], BF16) for i in range(3)]
            RB = [ph.sb(f"b_rb{i}", [64, NT]) for i in range(3)]
            EPS30 = ph.sb("b_eps30", [128, 1])
            GR = [ph.sb(f"b_gr{i}", [64, 9, NT]) for i in range(2)]
            NACC = ph.sb("b_nacc", [64, RPG, NT])
            TMP = [ph.sb(f"b_tmp{i}", [64, NT]) for i in range(3)]
            NOUT = [ph.sb(f"b_nout{i}", [64, NT], BF16) for i in range(2)]
            IMP = ph.sb("b_imp", [128, 4, 64])
            M8 = ph.sb("b_m8", [128, 4, 8])
            NM = ph.sb("b_nm", [128, 4, 128], BF16)
            RSC = ph.sb("b_rsc", [128, 4])
            PSs = [ph.ps(f"b_pss{i}", [128, NT]) for i in range(4)]
            PSo = [ph.ps(f"b_pso{i}", [128, NT]) for i in range(3)]
            PSX = ph.ps("b_psx", [128, NT])

            bC = Buf()
            bW1, bW2, bPET, bCB, bKCR = Buf(), Buf(), Buf(), Buf(), Buf()
            bHS = [Buf(), Buf()]
            bKCA, bVCA = Buf(), Buf()
            bKAlo = [Buf(), Buf()]; bKAhi = [Buf(), Buf()]; bKWlo = [Buf(), Buf()]; bKWhi = [Buf(), Buf()]
            bVS = [Buf(), Buf()]; bVW = [Buf(), Buf()]; bVone = [Buf(), Buf()]
            bQlo = [[Buf() for _ in range(RPG)] for _ in range(2)]
            bQhi = [[[Buf() for _ in range(NTT)] for _ in range(RPG)] for _ in range(2)]
            bPT = [Buf() for _ in PT]; bRB = [Buf() for _ in range(3)]
            bGR = [Buf(), Buf()]; bNACC = [Buf() for _ in range(RPG)]; bTMP = [Buf(), Buf(), Buf()]; bNOUT = [Buf(), Buf()]
            bIMP = [Buf() for _ in range(4)]; bM8 = [Buf() for _ in range(4)]; bNM = [Buf() for _ in range(4)]
            bRSC = Buf()
            bPSs = [Buf() for _ in range(4)]; bPSo = [Buf() for _ in range(3)]; bPSX = Buf()

            cst = self.cst

            def load_constants():
                for v0 in range(0, 13, 4):
                    v1 = min(13, v0 + 4)
                    kb.dma(pool, MASK[:, v0:v1, :], cst["c_mask"][:, v0:v1, :], writes=[bC])
                kb.dma(sp, BIAS[:], cst["c_bias"][:, :], writes=[bC])
                kb.dma(sp, IOTA[:], cst["c_iota"][:, :], writes=[bC])
                kb.dma(pool, OV[:], cst["c_ov"][:, :, :], writes=[bC])
                kb.dma(sp, FB[:], cst["c_fb"][:, :, :], writes=[bC])
                kb.dma(pool, IDENT[:], cst["c_ident"][:, :], writes=[bC])
                kb.dma(pool, IDENTBIG[:], cst["c_identbig"][:, :], writes=[bC])
            kb.op(dve, lambda: nc.vector.memset(ONESF[:], 1.0), writes=[bC])
            kb.op(dve, lambda: nc.vector.memset(EPS30[:], 1e-30), writes=[bC])
            kb.op(dve, lambda: nc.vector.memset(KCA[:], 0.0), writes=[bKCA])
            kb.op(dve, lambda: nc.vector.memset(KCA[64:65, :, :], 1.0), writes=[bKCA])
            kb.op(dve, lambda: nc.vector.memset(VCA[:], 0.0), writes=[bVCA])
            kb.op(dve, lambda: nc.vector.memset(VCA[:, :, :, 64:128], 1.0), writes=[bVCA])
            kb.op(dve, lambda: nc.vector.memset(NM[:], 0.0), writes=bNM)

            pss_i = [0]

            def npss():
                p = pss_i[0] % 4
                pss_i[0] += 1
                return p

            with Ph(kb) as ph1:
                W1s = [ph1.sb(f"b_w1_{i}", [128, 32, 256], BF16) for i in range(2)]
                W2 = ph1.sb("b_w2", [128, 2, 2, 64], BF16)
                PET = ph1.sb("b_pet", [128, 2, 32], BF16)
                CB = ph1.sb("b_cb", [128, 2, 2])
                KCRs = [ph1.sb(f"b_kcr{i}", [128, 2, T], BF16) for i in range(2)]
                KCR2 = ph1.sb("b_kcr2", [128, 2, 16, 256], BF16)
                HS = [ph1.sb(f"b_hs{i}", [128, 2, 256], BF16) for i in range(2)]
                bW1s = [[[Buf() for _ in range(4)] for _ in range(2)] for _ in range(2)]
                bKCRs = [Buf(), Buf()]; bKCR2 = Buf()
                for kv in range(2):
                    w1 = (self.w1k, self.w1v)[kv][l].rearrange("l d h -> d l h")
                    w2 = (self.w2k, self.w2v)[kv][l].rearrange("(hc p) d -> p hc d", p=128)
                    src = (self.KCRT, self.VCRT)[kv]
                    for half in range(2):
                        for l0 in range(0, 32, 8):
                            kb.dma(pool, W1s[kv][64 * half:64 * (half + 1), l0:l0 + 8, :], w1[:, l0:l0 + 8, :],
                                   writes=[bW1s[kv][half][l0 // 8]])
                    kb.dma(pool, W2[:, kv, :, :], w2, writes=[bW2])
                    kb.dma(pool, PET[:, kv, :], self.pe_t[:, l, kv, :], writes=[bPET])
                    kb.dma(sp, KCRs[kv][:], src.rearrange("(c p) t -> p c t", p=128), writes=[bKCRs[kv]])
                load_constants()
                for hb in range(2):
                    kb.op(dve, lambda hb=hb: nc.vector.memset(HS[hb][:], 0.0), writes=[bHS[hb]])
                for kv in range(2):
                    W1 = W1s[kv]
                    bW1 = bW1s[kv]
                    for c in range(2):
                        eng, eh = (act, nc.scalar) if c == 0 else (pool, nc.gpsimd)
                        if c == 0:
                            kb.op(act, lambda c=c: nc.scalar.copy(
                                out=KCR2[:, c, :, :], in_=KCRs[kv][:, c, :].rearrange("p (n s) -> p s n", s=16)),
                                reads=[bKCRs[kv]], writes=[bKCR2])
                        else:
                            kb.op(dve, lambda c=c: nc.vector.tensor_copy(
                                out=KCR2[:, c, :, :], in_=KCRs[kv][:, c, :].rearrange("p (n s) -> p s n", s=16)),
                                reads=[bKCRs[kv]], writes=[bKCR2])
                    for hc in range(2):
                        p = npss()
                        for ll in range(32):
                            kb.op(pe, lambda ll=ll, hc=hc, p=p: nc.tensor.matmul(
                                PSs[p][:, 0:1], lhsT=W1[0:64, ll, 128 * hc:128 * (hc + 1)], rhs=PET[0:64, kv, ll:ll + 1],
                                start=(ll == 0), stop=(ll == 31)), reads=[bW1[0][ll // 8], bPET], writes=[bPSs[p]])
                        kb.op(dve, lambda hc=hc, p=p: nc.vector.tensor_copy(out=CB[:, kv, hc:hc + 1], in_=PSs[p][:, 0:1]),
                              reads=[bPSs[p]], writes=[bCB])
                    for g in range(NKV):
                        pb = 64 * (g % 2)
                        ch = g // 2
                        hb = g % 2
                        for hc in range(2):
                            p = npss()
                            for ll in range(32):
                                n0 = 0 if ll < 16 else 1
                                kb.op(pe, lambda ll=ll, hc=hc, p=p, n0=n0: nc.tensor.matmul(
                                    PSs[p][:, 0:255], lhsT=W1[pb:pb + 64, ll, 128 * hc:128 * (hc + 1)],
                                    rhs=KCR2[pb:pb + 64, ch, ll % 16, n0:n0 + 255],
                                    start=(ll == 0), stop=(ll == 31)), reads=[bW1[g % 2][ll // 8], bKCR2], writes=[bPSs[p]])
                            kb.op(act, lambda hc=hc, p=p: nc.scalar.activation(
                                out=HS[hb][:, hc, 0:255], in_=PSs[p][:, 0:255], func=AF.Silu, bias=CB[:, kv, hc:hc + 1]),
                                reads=[bPSs[p], bCB], writes=[bHS[hb]])
                        if kv == 0:
                            p = npss()
                            for hc in range(2):
                                kb.op(pe, lambda hc=hc, p=p: nc.tensor.matmul(
                                    PSs[p][0:64, 0:255], lhsT=W2[:, 0, hc, :], rhs=HS[hb][:, hc, 0:255],
                                    start=(hc == 0), stop=(hc == 1)), reads=[bW2, bHS[hb]], writes=[bPSs[p]])
                            kb.op(dve, lambda p=p, g=g: nc.vector.tensor_copy(out=KCA[0:64, g, 0:255], in_=PSs[p][0:64, 0:255]),
                                  reads=[bPSs[p]], writes=[bKCA])
                        else:
                            for ntile in range(2):
                                p = npss()
                                for hc in range(2):
                                    kb.op(pe, lambda hc=hc, p=p, ntile=ntile: nc.tensor.matmul(
                                        PSs[p][:, 0:64], lhsT=HS[hb][:, hc, 128 * ntile:128 * (ntile + 1)], rhs=W2[:, 1, hc, :],
                                        start=(hc == 0), stop=(hc == 1)), reads=[bW2, bHS[hb]], writes=[bPSs[p]])
                                kb.op(dve, lambda p=p, g=g, ntile=ntile: nc.vector.tensor_copy(
                                    out=VCA[:, g, ntile, 0:64], in_=PSs[p][:, 0:64]), reads=[bPSs[p]], writes=[bVCA])

            KA = [ph.sb(f"b_ka{i}", [128, T], BF16) for i in range(2)]
            KW = [ph.sb(f"b_kw{i}", [128, T], BF16) for i in range(2)]
            VS = [ph.sb(f"b_vs{i}", [128, 32, 128], BF16) for i in range(2)]
            VW = [ph.sb(f"b_vw{i}", [128, 32, 128], BF16) for i in range(2)]
            QA = [ph.sb(f"b_qa{i}", [128, RPG, T], BF16) for i in range(2)]
            for st in range(2):
                kb.dma(pool, KA[st][64:128, :], cst["c_E"][:, :], writes=[bKAhi[st]])
                kb.op(dve, lambda st=st: nc.vector.memset(KW[st][64:65, :], 1.0), writes=[bKWhi[st]])
                kb.op(dve, lambda st=st: nc.vector.memset(VS[st][:, :, 64:128], 1.0), writes=[bVone[st]])
                kb.op(dve, lambda st=st: nc.vector.memset(VW[st][:, :, 64:128], 1.0), writes=[bVone[st]])
            if "KCA" in self.debug:
                kb.dma(sp, self.dbg_kca[:, :, :], KCA[0:64, :, :], reads=[bKCA])
                kb.dma(sp, self.dbg_vca[:, :, :, :], VCA[:, :, :, 0:64], reads=[bVCA])

            ctr = {"pso": 0, "pt": 0, "osb": 0, "bc": 0, "gr": 0, "tmp": 0, "nout": 0}

            def nx(key, n):
                v = ctr[key] % n
                ctr[key] += 1
                return v

            stream = []
            deferred = []

            def add(**kw):
                stream.append(kw)

            def fin_stage1(ob, r, br, gi):
                def stage():
                    o2 = nx("osb", 3)
                    if br != 1:
                        kb.op(act, lambda: nc.scalar.activation(out=RB[o2][:, :], in_=PSo[ob][64:128, :], func=AF.Ln,
                                                                bias=EPS30[64:128, 0:1]), reads=[bPSo[ob], bC], writes=[bRB[o2]])
                        kb.op(act, lambda: nc.scalar.activation(out=RB[o2][:, :], in_=RB[o2][:, :], func=AF.Exp, scale=-1.0),
                              reads=[bRB[o2]], writes=[bRB[o2]])
                    else:
                        kb.op(dve, lambda: nc.vector.reciprocal(out=RB[o2][:, :], in_=PSo[ob][64:128, :]),
                              reads=[bPSo[ob]], writes=[bRB[o2]])
                    t2 = nx("tmp", 3)
                    kb.op(dve, lambda: nc.vector.tensor_tensor(out=TMP[t2][:], in0=PSo[ob][0:64, :], in1=RB[o2][:, :],
                                                               op=ALU.mult), reads=[bPSo[ob], bRB[o2]], writes=[bTMP[t2]])
                    if br == 0:
                        kb.op(pool, lambda: nc.gpsimd.tensor_tensor(out=NACC[:, r, :], in0=TMP[t2][:],
                                                                    in1=GR[gi][0:64, 3 * r + br, :], op=ALU.mult),
                              reads=[bTMP[t2], bGR[gi]], writes=[bNACC[r]])
                    else:
                        kb.op(pool, lambda: nc.gpsimd.tensor_tensor(out=TMP[t2][:], in0=TMP[t2][:],
                                                                    in1=GR[gi][0:64, 3 * r + br, :], op=ALU.mult),
                              reads=[bTMP[t2], bGR[gi]], writes=[bTMP[t2]])
                        kb.op(pool, lambda: nc.gpsimd.tensor_tensor(out=NACC[:, r, :], in0=NACC[:, r, :], in1=TMP[t2][:],
                                                                    op=ALU.add),
                              reads=[bTMP[t2], bNACC[r]], writes=[bNACC[r]])
                return stage

            vsv = self.VSd.rearrange("(kt p) (g d) -> p kt g d", p=128, g=NKV)
            vwv = self.VWd.rearrange("(kt p) (g d) -> p kt g d", p=128, g=NKV)

            def group_loads(g, st):
                def f():
                    kb.dma(sp, KA[st][0:64, :], self.KST[64 * g:64 * (g + 1), :], writes=[bKAlo[st]])
                    kb.dma(sp, KW[st][0:64, :], self.KWT[64 * g:64 * (g + 1), :], writes=[bKWlo[st]])
                    for k0 in range(0, 32, 8):
                        kb.dma(sp, VS[st][:, k0:k0 + 8, 0:64], vsv[:, k0:k0 + 8, g, :], writes=[bVS[st]])
                        kb.dma(sp, VW[st][:, k0:k0 + 8, 0:64], vwv[:, k0:k0 + 8, g, :], writes=[bVW[st]])
                    qv = self.QT[192 * g:192 * (g + 1), :].rearrange("(r d) t -> d r t", r=RPG)
                    for r in range(RPG):
                        kb.dma(sp, QA[st][0:64, r, :], qv[:, r, :], writes=[bQlo[st][r]])
                        h = 3 * g + r
                        kb.dma(sp, QA[st][64:65, r, :], cst["c_cq"][h:h + 1, :], writes=[bQhi[st][r][i] for i in qtiles])
                return f

            def run_stream():
                DEPTH_ = 3
                nb = len(stream)
                _run(DEPTH_, nb)
                stream[:] = []

            def _run(DEPTH_, nb):
              if True:
                def issue_qk(b):
                    blk = stream[b]
                    if blk["pre"] is not None:
                        blk["pre"]()
                    p = npss()
                    blk["p"] = p
                    blk["qk"](p)

                for b in range(min(DEPTH_, nb)):
                    issue_qk(b)
                for b in range(nb):
                    if b + DEPTH_ < nb:
                        issue_qk(b + DEPTH_)
                    blk = stream[b]
                    p = blk["p"]
                    pt = nx("pt", 3)
                    col = blk["col"]
                    e0, e1 = blk.get("cr", (0, NT))
                    kb.op(act, lambda p=p, pt=pt, col=col, e0=e0, e1=e1: nc.scalar.activation(
                        out=PT[pt][:, e0:e1], in_=PSs[p][:, e0:e1], func=AF.Exp, bias=BIAS[:, col:col + 1]),
                        reads=[bPSs[p], bC], writes=[bPT[pt]])
                    blk["pv"](pt)
                    keep = []
                    for due, fn in deferred:
                        if due <= b:
                            fn()
                        else:
                            keep.append((due, fn))
                    deferred[:] = keep
                    if blk["post"] is not None:
                        blk["post"](b)
                for due, fn in deferred:
                    fn()
                deferred[:] = []

            glist = list(groups)
            qlist = list(qtiles)
            for gidx, g in enumerate(glist):
                st = gidx % 2
                for ti, i in enumerate(qlist):
                    qs = slice(i * NT, (i + 1) * NT)
                    gi = nx("gr", 2)
                    kts = [0] if i <= 3 else [0, 1]
                    pf = []
                    if gidx == 0 and ti == 0:
                        pf.append(group_loads(g, st))
                    if gidx + 1 < len(glist) and ti == min(1, len(qlist) - 1):
                        pf.append(group_loads(glist[gidx + 1], 1 - st))

                    def tile_pre(g=g, i=i, gi=gi, qs=qs, pf=pf):
                        for f_ in pf:
                            f_()
                        kb.dma(sp, GR[gi][:, :, :],
                               self.GT[9 * g:9 * (g + 1), qs].rearrange("(o r) t -> o r t", o=1).to_broadcast([64, 9, NT]),
                               writes=[bGR[gi]])
                    first_blk = True

                    for r in range(RPG):
                        h = 3 * g + r
                        ob_box = {}
                        for kt in kts:
                            midx = None
                            if kt == 0 and i <= 4:
                                midx = 8 + i
                            elif kt == 1:
                                midx = 8 + (i - 4)

                            def qk(p, kt=kt, r=r, midx=midx, g=g, qs=qs, i=i, st=st):
                                kb.op(pe, lambda: nc.tensor.matmul(
                                    PSs[p][:], lhsT=KCA[0:65, g, 128 * kt:128 * (kt + 1)], rhs=QA[st][0:65, r, qs],
                                    start=True, stop=(midx is None)), reads=[bKCA, bQlo[st][r], bQhi[st][r][i]],
                                    writes=[bPSs[p]])
                                if midx is not None:
                                    kb.op(pe, lambda: nc.tensor.matmul(
                                        PSs[p][:], lhsT=IDENT[:], rhs=MASK[:, midx, :], start=False, stop=True),
                                        reads=[bC], writes=[bPSs[p]])
                            col = 420 + (h * 2 + kt) * 8 + i

                            def pv(pt, kt=kt, g=g, ob_box=ob_box, first=(kt == kts[0]), last=(kt == kts[-1])):
                                if first:
                                    ob_box["ob"] = nx("pso", 3)
                                ob = ob_box["ob"]
                                kb.op(pe, lambda: nc.tensor.matmul(
                                    PSo[ob][:, :], lhsT=VCA[:, g, kt, :], rhs=PT[pt][:], start=first, stop=last),
                                    reads=[bVCA, bPT[pt]], writes=[bPSo[ob]])
                                ob_box.setdefault("pts", []).append((kt, pt))
                                if last:
                                    pts = ob_box["pts"]
                                    for s4 in range(4):
                                        for n_, (kt_, pt_) in enumerate(pts):
                                            kb.op(pe, lambda s4=s4, kt_=kt_, pt_=pt_, n_=n_: nc.tensor.matmul(
                                                PSX[:, 128 * s4:128 * s4 + 65], lhsT=PT[pt_][:, 128 * s4:128 * (s4 + 1)],
                                                rhs=OV[:, kt_, :], start=(n_ == 0), stop=(n_ == len(pts) - 1)),
                                                reads=[bPT[pt_], bC], writes=[bPSX])
                            post = None
                            if kt == kts[-1]:
                                def post(b, r=r, i=i, ob_box=ob_box, gi=gi):
                                    psx3 = PSX[:, :].rearrange("p (s c) -> p s c", c=128)
                                    kb.op(dve, lambda: nc.vector.tensor_scalar(out=RSC[:, :], in0=psx3[:, :, 64], scalar1=1e-30,
                                                                               scalar2=None, op0=ALU.max),
                                          reads=[bPSX], writes=[bRSC])
                                    kb.op(dve, lambda: nc.vector.reciprocal(out=RSC[:, :], in_=RSC[:, :]),
                                          reads=[bRSC], writes=[bRSC])
                                    for s4 in range(4):
                                        in1 = FB[:, 4 * i + s4, :] if r == 0 else IMP[:, s4, :]
                                        kb.op(dve, lambda s4=s4, in1=in1: nc.vector.scalar_tensor_tensor(
                                            out=IMP[:, s4, :], in0=PSX[:, 128 * s4:128 * s4 + 64], scalar=RSC[:, s4:s4 + 1],
                                            in1=in1, op0=ALU.mult, op1=ALU.add),
                                            reads=[bPSX, bRSC, bC, bIMP[s4]], writes=[bIMP[s4]])
                                    deferred.append((b + 2, fin_stage1(ob_box["ob"], r, 0, gi)))
                            add(pre=(tile_pre if first_blk else None), qk=qk, col=col, pv=pv, post=post)
                            first_blk = False

                    def topk(g=g, i=i, qs=qs, st=st):
                        for s4 in range(4):
                            kb.op(dve, lambda s4=s4: nc.vector.max(out=M8[:, s4, :], in_=IMP[:, s4, :]),
                                  reads=[bIMP[s4]], writes=[bM8[s4]])
                            kb.op(dve, lambda s4=s4: nc.vector.tensor_scalar(
                                out=NM[:, s4, 64:128], in0=IMP[:, s4, :], scalar1=M8[:, s4, 7:8], scalar2=1.0,
                                op0=ALU.is_ge, op1=ALU.subtract), reads=[bIMP[s4], bM8[s4]], writes=[bNM[s4]])
                        for s4 in range(4):
                            kb.op(pe, lambda s4=s4: nc.tensor.matmul(PSX[:, 128 * s4:128 * (s4 + 1)], lhsT=NM[:, s4, :],
                                                                     rhs=IDENTBIG[:], start=True, stop=True),
                                  reads=[bNM[s4], bC], writes=[bPSX])
                        for r in range(RPG):
                            kb.op(dve, lambda r=r: nc.vector.scalar_tensor_tensor(
                                out=QA[st][64:128, r, qs], in0=IOTA[64:128, :], scalar=-SLOPES[3 * g + r], in1=PSX[64:128, :],
                                op0=ALU.mult, op1=ALU.add), reads=[bC, bPSX], writes=[bQhi[st][r][i]])

                    def win_blocks(r):
                        h = 3 * g + r
                        vlist = [4, 5, 6, 7] if i == 0 else [3, 4, 0, 1, 2, 5, 6, 7]
                        ob_box = {}
                        for v in vlist:
                            kt = 4 * i - 4 + v
                            if v < 4:
                                c0, c1, m0, midx = 0, 128 * (v + 1), 128 * v, 4 + v
                            else:
                                c0, c1, m0, midx = 128 * (v - 4), NT, 128 * (v - 4), v - 4

                            def qk(p, kt=kt, r=r, midx=midx, qs=qs, i=i, st=st, c0=c0, c1=c1, m0=m0):
                                kb.op(pe, lambda: nc.tensor.matmul(
                                    PSs[p][:, c0:c1], lhsT=KW[st][0:65, 128 * kt:128 * (kt + 1)],
                                    rhs=QA[st][0:65, r, qs.start + c0:qs.start + c1],
                                    start=True, stop=False), reads=[bKWlo[st], bKWhi[st], bQlo[st][r], bQhi[st][r][i]],
                                    writes=[bPSs[p]])
                                kb.op(pe, lambda: nc.tensor.matmul(
                                    PSs[p][:, m0:m0 + 128], lhsT=IDENT[:], rhs=MASK[:, midx, m0:m0 + 128],
                                    start=False, stop=True), reads=[bC], writes=[bPSs[p]])
                            col = h * 35 + (v - 4 + 31)

                            def pv(pt, kt=kt, ob_box=ob_box, first=(v == vlist[0]), last=(v == vlist[-1]), st=st,
                                   c0=c0, c1=c1):
                                if first:
                                    ob_box["ob"] = nx("pso", 3)
                                ob = ob_box["ob"]
                                kb.op(pe, lambda: nc.tensor.matmul(
                                    PSo[ob][:, c0:c1], lhsT=VW[st][:, kt, :], rhs=PT[pt][:, c0:c1], start=first, stop=last),
                                    reads=[bVW[st], bVone[st], bPT[pt]], writes=[bPSo[ob]])
                            post = None
                            if v == vlist[-1]:
                                def post(b, r=r, ob_box=ob_box, gi=gi):
                                    deferred.append((b + 2, fin_stage1(ob_box["ob"], r, 2, gi)))
                            add(pre=None, qk=qk, col=col, pv=pv, post=post, cr=(c0, c1))

                    def sel_blocks(r, pre0):
                        h = 3 * g + r
                        nk = 4 * i + 4
                        kt_first = sel_first_tile(h, i)
                        ob_box = {}
                        for kt in range(kt_first, nk):
                            diag = kt >= 4 * i

                            c0 = 128 * (kt - 4 * i) if diag else 0

                            def qk(p, kt=kt, r=r, diag=diag, qs=qs, i=i, st=st, c0=c0):
                                kb.op(pe, lambda: nc.tensor.matmul(
                                    PSs[p][:, c0:NT], lhsT=KA[st][:, 128 * kt:128 * (kt + 1)],
                                    rhs=QA[st][:, r, qs.start + c0:qs.stop],
                                    start=True, stop=(not diag)), reads=[bKAlo[st], bKAhi[st], bQlo[st][r], bQhi[st][r][i]],
                                    writes=[bPSs[p]])
                                if diag:
                                    kb.op(pe, lambda: nc.tensor.matmul(
                                        PSs[p][:, c0:c0 + 128], lhsT=IDENT[:], rhs=MASK[:, kt - 4 * i, c0:c0 + 128],
                                        start=False, stop=True), reads=[bC], writes=[bPSs[p]])
                            col = h * 35 + (kt - 4 * i + 31)

                            def pv(pt, kt=kt, ob_box=ob_box, first=(kt == kt_first), last=(kt == nk - 1), st=st, c0=c0):
                                if first:
                                    ob_box["ob"] = nx("pso", 3)
                                ob = ob_box["ob"]
                                kb.op(pe, lambda: nc.tensor.matmul(
                                    PSo[ob][:, c0:NT], lhsT=VS[st][:, kt, :], rhs=PT[pt][:, c0:NT], start=first, stop=last),
                                    reads=[bVS[st], bVone[st], bPT[pt]], writes=[bPSo[ob]])
                            post = None
                            if kt == nk - 1:
                                def post(b, r=r, ob_box=ob_box, gi=gi, h=h, qs=qs, i=i):
                                    st2 = fin_stage1(ob_box["ob"], r, 1, gi)

                                    def st2b():
                                        st2()
                                        no = nx("nout", 2)
                                        kb.op(pool, lambda: nc.gpsimd.tensor_copy(out=NOUT[no][:], in_=NACC[:, r, :]),
                                              reads=[bNACC[r]], writes=[bNOUT[no]])
                                        kb.dma(pool, self.MIXT[CC + 64 * h:CC + 64 * (h + 1), qs], NOUT[no][:],
                                               reads=[bNOUT[no]], writes=[self.b_mix[i]])
                                    deferred.append((b + 2, st2b))
                            add(pre=(pre0 if kt == kt_first else None), qk=qk, col=col, pv=pv, post=post, cr=(c0, NT))

                    win_blocks(0)
                    sel_blocks(0, topk)
                    win_blocks(1)
                    sel_blocks(1, None)
                    win_blocks(2)
                    sel_blocks(2, None)
            run_stream()


def build(n_layers=DEPTH, phases=("a", "b", "c"), debug=(), ext_in=(), bkw={}):
    nc = bass.Bass("TRN2", target_bir_lowering=False)
    es = contextlib.ExitStack()
    with es:
        pg = Prog(nc, es, debug=debug, ext_in=ext_in)
        for l in range(n_layers):
            if "a" in phases:
                pg.phase_a(l)
            if "b0" in phases:
                pg.phase_b0(l)
            if "b12" in phases or "b" in phases:
                pg.phase_b12(l, **bkw)
            if "c" in phases:
                pg.phase_c(l, last=(l == n_layers - 1))
        pg.kb.barrier()
    return nc, pg


def make_in_maps(inputs):
    cst = host_constants()
    sm, fin, pe = host_small(inputs)
    x = np.asarray(inputs["x"])
    shared = {
        "w_in": np.asarray(inputs["w_in"]), "cmp_k_w1": np.asarray(inputs["cmp_k_w1"]),
        "cmp_k_w2": np.asarray(inputs["cmp_k_w2"]), "cmp_v_w1": np.asarray(inputs["cmp_v_w1"]),
        "cmp_v_w2": np.asarray(inputs["cmp_v_w2"]), "w_out": np.asarray(inputs["w_out"]),
        "w_gate_up": np.asarray(inputs["w_gate_up"]), "w_down": np.asarray(inputs["w_down"]),
        "c_small": sm, "c_fin": fin, "c_pe": pe,
    }
    shared.update(cst)
    maps = []
    for b in range(x.shape[0]):
        m = dict(shared)
        m["xT"] = np.ascontiguousarray(x[b].T)
        maps.append(m)
    return maps


def kernel(**inputs):
    nc, pg = build()
    maps = make_in_maps(inputs)
    res = run_bass_kernel_spmd(nc, maps, core_ids=list(range(len(maps))))
    out = np.stack([np.ascontiguousarray(r["outT"].T) for r in res.results], axis=0)
    return out.astype(np.float32)
```

```python
import contextlib
import math
import numpy as np
import concourse.bass as bass
import concourse.mybir as mybir
from concourse.bass_utils import run_bass_kernel_spmd

F32 = mybir.dt.float32
BF16 = mybir.dt.bfloat16
AF = mybir.ActivationFunctionType
ALU = mybir.AluOpType

D = 1024
T = 4096
DEPTH = 4
HD = 64
CC = 256
CW = 31
NQ = 12
NKV = 4
RPG = 3
DFF = 2816
INC = 2852
EPS = 1e-6
NT = 512
NTT = T // NT
KC = D // 128
FC = DFF // 128
BIG = 30000.0
NSM = 84


def alibi_slopes(n):
    def p2(m):
        start = 2.0 ** (-8.0 / m)
        return [start ** (i + 1) for i in range(m)]
    if math.log2(n).is_integer():
        s = p2(n)
    else:
        c = 2 ** math.floor(math.log2(n))
        s = p2(c) + p2(2 * c)[0::2][: n - c]
    return [float(np.float32(v)) for v in s]


SLOPES = alibi_slopes(NQ)
ALIBI_CUT = 300.0


def sel_first_tile(h, i):
    kt0 = 0
    for kt in range(4 * i):
        dmin = 128 * (4 * i - kt) - 127
        if SLOPES[h] * dmin > ALIBI_CUT:
            kt0 = kt + 1
    return kt0
CC_DELTAS = [31, -481, -993, -1505, -2017]


class Sem:
    __slots__ = ("h",)

    def __init__(self, h):
        self.h = h


class Buf:
    __slots__ = ("w", "r")

    def __init__(self):
        self.w = {}
        self.r = {}


class Slot:
    __slots__ = ("sem", "val")

    def __init__(self, sem):
        self.sem = sem
        self.val = 0


class Eng:
    def __init__(self, kb, name, h, nslots=0):
        self.kb = kb
        self.name = name
        self.h = h
        self.sem = kb.newsem(name)
        self.cnt = 0
        self.seen = {}
        self.slots = [Slot(kb.newsem(f"{name}d{i}")) for i in range(nslots)]
        self.rr = 0


class KB:
    def __init__(self, nc, es):
        self.nc = nc
        self.es = es
        self.nsem = 0
        self.pe = Eng(self, "pe", nc.tensor)
        self.act = Eng(self, "act", nc.scalar, nslots=4)
        self.dve = Eng(self, "dve", nc.vector)
        self.pool = Eng(self, "pool", nc.gpsimd, nslots=8)
        self.sp = Eng(self, "sp", nc.sync, nslots=12)
        self.engs = [self.pe, self.act, self.dve, self.pool, self.sp]
        self.ninst = 0

    def newsem(self, name):
        self.nsem += 1
        return Sem(self.es.enter_context(self.nc.semaphore(f"s{self.nsem}_{name}")))

    def _wait(self, eng, deps):
        for sem, val in deps.items():
            if val > 0 and eng.seen.get(sem, 0) < val:
                eng.h.wait_ge(sem.h, val)
                eng.seen[sem] = val

    def _deps(self, eng, reads, writes):
        d = {}
        for b in reads:
            for s, v in b.w.items():
                if d.get(s, 0) < v:
                    d[s] = v
        for b in writes:
            for s, v in b.w.items():
                if d.get(s, 0) < v:
                    d[s] = v
            for s, v in b.r.items():
                if d.get(s, 0) < v:
                    d[s] = v
        if eng is self.pe:
            d.pop(self.pe.sem, None)
        return d

    def op(self, eng, fn, reads=(), writes=()):
        self._wait(eng, self._deps(eng, reads, writes))
        ins = fn()
        eng.cnt += 1
        self.ninst += 1
        ins.then_inc(eng.sem.h, 1)
        for b in reads:
            if b.r.get(eng.sem, 0) < eng.cnt:
                b.r[eng.sem] = eng.cnt
        for b in writes:
            b.w = {eng.sem: eng.cnt}
            b.r = {}

    def dma(self, q, out, in_, reads=(), writes=(), **kw):
        slot = q.slots[q.rr % len(q.slots)]
        q.rr += 1
        d = self._deps(q, reads, writes)
        if slot.val > 0 and d.get(slot.sem, 0) < slot.val:
            d[slot.sem] = slot.val
        self._wait(q, d)
        slot.val += 16
        self.ninst += 1
        q.h.dma_start(out=out, in_=in_, **kw).then_inc(slot.sem.h, 16)
        for b in reads:
            if b.r.get(slot.sem, 0) < slot.val:
                b.r[slot.sem] = slot.val
        for b in writes:
            b.w = {slot.sem: slot.val}
            b.r = {}

    def barrier(self):
        d = {}
        for e in self.engs:
            if e.cnt > 0:
                d[e.sem] = e.cnt
            for s in e.slots:
                if s.val > 0:
                    d[s.sem] = s.val
        for e in self.engs:
            self._wait(e, dict(d))
        for e in self.engs:
            if e.cnt > 30000:
                e.sem = self.newsem(e.name)
                e.cnt = 0


class Ph:
    _n = [0]

    def __init__(self, kb):
        self.kb = kb
        self.es = contextlib.ExitStack()
        Ph._n[0] += 1
        self.pfx = f"p{Ph._n[0]}_"

    def __enter__(self):
        self.es.__enter__()
        return self

    def __exit__(self, *a):
        self.kb.barrier()
        return self.es.__exit__(*a)

    def sb(self, name, shape, dt=F32):
        return self.es.enter_context(self.kb.nc.sbuf_tensor(self.pfx + name, list(shape), dt))

    def ps(self, name, shape, dt=F32):
        return self.es.enter_context(self.kb.nc.psum_tensor(self.pfx + name, list(shape), dt))


def host_constants():
    c = {}
    k = np.arange(128)[:, None]
    q = np.arange(512)[None, :]
    masks = []
    for v in range(4):
        masks.append(np.where(128 * v + k > q, -BIG, 0.0))
    for v in range(4):
        masks.append(np.where(q - k < 128 * v, 0.0, -BIG))
    for dl in CC_DELTAS:
        masks.append(np.where(16 * k + dl > q, -BIG, 0.0))
    c["c_mask"] = np.stack(masks, axis=1).astype(np.float32)
    pos = np.arange(T)
    c["c_E"] = (pos[None, :] // 64 == np.arange(64)[:, None]).astype(np.float32)
    sl = np.asarray(SLOPES, dtype=np.float64)
    dl = np.arange(-31, 4)
    b1 = sl[None, :, None] * (np.arange(128)[:, None, None] + 128.0 * dl[None, None, :])
    kt = np.arange(2)
    it = np.arange(8)
    b2 = sl[None, :, None, None] * (16.0 * np.arange(128)[:, None, None, None] + 15.5
                                    + 2048.0 * kt[None, None, :, None] - 512.0 * it[None, None, None, :])
    c["c_bias"] = np.concatenate([b1.reshape(128, -1), b2.reshape(128, -1)], axis=1).astype(np.float32)
    c["c_iota"] = np.broadcast_to(np.arange(512, dtype=np.float32)[None, :], (128, 512)).copy()
    n = np.arange(256)
    j = np.arange(64)
    cs = n[:, None] * 16
    ov = ((cs < j[None, :] * 64 + 64) & (cs + 32 > j[None, :] * 64)).astype(np.float32)
    ov[255, :] = 0.0
    ov = np.concatenate([ov, np.ones((256, 1), np.float32)], axis=1)
    c["c_ov"] = ov.reshape(2, 128, 65).transpose(1, 0, 2).copy()
    tq = np.arange(T)
    cur = tq // 64
    forced = (j[None, :] == 0) | (j[None, :] == cur[:, None]) | (j[None, :] == cur[:, None] - 1)
    fb = np.where(j[None, :] <= cur[:, None], 1000.0 * forced, -1e9).astype(np.float32)
    c["c_fb"] = fb.reshape(32, 128, 64).transpose(1, 0, 2).copy()
    import ml_dtypes
    qoff = (np.arange(T) % NT).astype(np.float32)
    c["c_cq"] = (np.asarray(SLOPES, np.float32)[:, None] * -qoff[None, :]).astype(np.float32).astype(ml_dtypes.bfloat16)
    c["c_ident"] = np.eye(128, dtype=np.float32)
    c["c_identbig"] = (np.eye(128) * BIG).astype(np.float32)
    return c


CONST_SHAPES = {
    "c_mask": [128, 13, 512], "c_E": [64, T], "c_bias": [128, 612], "c_iota": [128, 512],
    "c_ov": [128, 2, 65], "c_fb": [128, 32, 64], "c_ident": [128, 128], "c_identbig": [128, 128],
}


def host_small(inp):
    sm = np.zeros((128, DEPTH, NSM), np.float32)
    for l in range(DEPTH):
        sm[:, l, 0:8] = np.asarray(inp["attn_norm"][l]).reshape(8, 128).T
        sm[:, l, 8:16] = np.asarray(inp["ffn_norm"][l]).reshape(8, 128).T
        cw = np.asarray(inp["conv_w"][l])
        for c in range(2):
            sm[:, l, 16 + 31 * c: 16 + 31 * (c + 1)] = cw[:, 128 * c: 128 * (c + 1)].T
        sm[:, l, 78:80] = np.asarray(inp["conv_b"][l]).reshape(2, 128).T
        sm[:, l, 80:82] = np.asarray(inp["conv_ln_g"][l]).reshape(2, 128).T
        sm[:, l, 82:84] = np.asarray(inp["conv_ln_b"][l]).reshape(2, 128).T
    fin = np.asarray(inp["final_norm"]).reshape(8, 128).T.copy()
    pe = np.zeros((128, DEPTH, 2, 32), np.float32)
    for l in range(DEPTH):
        for kv, nm in enumerate(("cmp_k_pe", "cmp_v_pe")):
            p = np.asarray(inp[nm][l]).T
            pe[0:64, l, kv] = p
            pe[64:128, l, kv] = p
    return sm, fin, pe


class Prog:
    def __init__(self, nc, es, debug=(), ext_in=()):
        self.nc = nc
        self.kb = KB(nc, es)
        self.debug = set(debug)
        self.ext_in = set(ext_in)
        nc_ = nc

        def inp(name, shape, dt=F32):
            return nc_.dram_tensor(name, list(shape), dt, kind="ExternalInput").ap()

        self.xin = inp("xT", [D, T])
        self.w_in = inp("w_in", [DEPTH, D, INC])
        self.w1k = inp("cmp_k_w1", [DEPTH, 32, 64, 256])
        self.w2k = inp("cmp_k_w2", [DEPTH, 256, 64])
        self.w1v = inp("cmp_v_w1", [DEPTH, 32, 64, 256])
        self.w2v = inp("cmp_v_w2", [DEPTH, 256, 64])
        self.w_out = inp("w_out", [DEPTH, D, D])
        self.w_gu = inp("w_gate_up", [DEPTH, D, 2 * DFF])
        self.w_dn = inp("w_down", [DEPTH, DFF, D])
        self.small = inp("c_small", [128, DEPTH, NSM])
        self.fin = inp("c_fin", [128, 8])
        self.pe_t = inp("c_pe", [128, DEPTH, 2, 32])
        self.cst = {k: inp(k, v) for k, v in CONST_SHAPES.items()}
        self.cst["c_cq"] = inp("c_cq", [NQ, T], BF16)

        self.outT = nc.dram_tensor("outT", [D, T], F32, kind="ExternalOutput").ap()
        self.xres = self.scr("xres", [D, T], F32)
        self.YT = self.scr("YT", [CC, T], F32)
        self.QT = self.scr("QT", [768, T], BF16)
        self.KCRT = self.scr("KCRT", [256, T], BF16)
        self.VCRT = self.scr("VCRT", [256, T], BF16)
        self.KST = self.scr("KST", [256, T], BF16)
        self.KWT = self.scr("KWT", [256, T], BF16)
        self.VSd = self.scr("VSd", [T, 256], BF16)
        self.VWd = self.scr("VWd", [T, 256], BF16)
        self.GT = self.scr("GT", [36, T], F32)
        self.MIXT = self.scr("MIXT", [D, T], BF16)
        if "KCA" in self.debug:
            self.dbg_kca = nc.dram_tensor("dbg_kca", [64, NKV, 256], BF16, kind="ExternalOutput").ap()
            self.dbg_vca = nc.dram_tensor("dbg_vca", [128, NKV, 2, 64], BF16, kind="ExternalOutput").ap()
        self.b_x = [Buf() for _ in range(NTT)]
        self.b_a = [Buf() for _ in range(NTT)]
        self.b_mix = [Buf() for _ in range(NTT)]

    def scr(self, name, shape, dt):
        kind = "ExternalOutput" if name in self.debug else ("ExternalInput" if name in self.ext_in else "Internal")
        return self.nc.dram_tensor(name, list(shape), dt, kind=kind).ap()

    def phase_a(self, l):
        kb, nc = self.kb, self.nc
        pe, act, dve, pool, sp = kb.pe, kb.act, kb.dve, kb.pool, kb.sp
        xsrc = self.xin if l == 0 else self.xres
        xv = xsrc.rearrange("(kc p) t -> p kc t", p=128)
        with Ph(kb) as ph:
            WIN = ph.sb("a_win", [128, KC, INC], BF16)
            SM = ph.sb("a_sm", [128, NSM])
            ONESB = ph.sb("a_ones", [128, 128], BF16)
            EPSC = ph.sb("a_epsc", [128, 1])
            XT = [ph.sb(f"a_xt{i}", [128, KC, NT]) for i in range(2)]
            SQ = [ph.sb(f"a_sq{i}", [128, KC, NT], BF16) for i in range(2)]
            XS = [ph.sb(f"a_xs{i}", [128, KC, NT], BF16) for i in range(2)]
            RS = [ph.sb(f"a_rs{i}", [128, NT]) for i in range(2)]
            EV = [ph.sb(f"a_ev{i}", [128, NT], BF16) for i in range(4)]
            EF = [ph.sb(f"a_ef{i}", [128, NT]) for i in range(2)]
            SG = [ph.sb(f"a_sg{i}", [128, NT]) for i in range(2)]
            EVV = [ph.sb(f"a_evv{i}", [128, 512], BF16) for i in range(2)]
            PS = [ph.ps(f"a_ps{i}", [128, NT]) for i in range(6)]
            PSS = ph.ps("a_pss", [128, NT])
            YPT = [[ph.sb(f"a_yp{c}_{k}", [128, CW - 1 + NT]) for k in range(2)] for c in range(2)]
            CACC = [ph.sb(f"a_cacc{c}", [128, NT]) for c in range(2)]
            CAB = [ph.sb(f"a_cab{c}", [128, NT], BF16) for c in range(2)]
            CSQ = [ph.sb(f"a_csq{c}", [128, NT], BF16) for c in range(2)]
            CMEAN = ph.sb("a_cmean", [128, NT])
            CVAR = ph.sb("a_cvar", [128, NT])
            CDT = [ph.sb(f"a_cdt{c}", [128, NT]) for c in range(2)]
            COB = [ph.sb(f"a_cob{c}", [128, NT], BF16) for c in range(2)]
            bYPT = [[Buf(), Buf()], [Buf(), Buf()]]
            bCACC = [Buf(), Buf()]; bCAB = [Buf(), Buf()]; bCSQ = [Buf(), Buf()]
            bCMEAN, bCVAR = Buf(), Buf()
            bCDT = [Buf(), Buf()]; bCOB = [Buf(), Buf()]
            bWIN, bSM, bONES, bPSS = Buf(), Buf(), Buf(), Buf()
            bXT = [Buf(), Buf()]; bSQ = [Buf(), Buf()]; bXS = [Buf(), Buf()]; bRS = [Buf(), Buf()]
            bEV = [Buf() for _ in EV]; bEF = [Buf() for _ in EF]; bSG = [Buf() for _ in SG]
            bEVV = [Buf() for _ in EVV]; bPS = [Buf() for _ in PS]

            kb.dma(sp, SM[:], self.small[:, l, :], writes=[bSM])
            kb.op(dve, lambda: nc.vector.memset(ONESB[:], 1.0), writes=[bONES])
            kb.op(dve, lambda: nc.vector.memset(EPSC[:], EPS), writes=[bONES])
            for c in range(2):
                kb.op(dve, lambda c=c: nc.vector.memset(YPT[c][0][:, 0:CW - 1], 0.0), writes=[bYPT[c][0]])
            wv = self.w_in[l].rearrange("(kc p) n -> p kc n", p=128)
            wcols = [(0, 512), (512, 1280), (1280, 2048), (2048, INC)]
            bWINc = [Buf() for _ in wcols]
            for (c0_, c1_), bw in zip(wcols, bWINc):
                for k0 in range(0, KC, 4):
                    kb.dma(pool, WIN[:, k0:k0 + 4, c0_:c1_], wv[:, k0:k0 + 4, c0_:c1_], writes=[bw])

            def bwin(col):
                for (c0_, c1_), bw in zip(wcols, bWINc):
                    if c0_ <= col < c1_:
                        return bw
                raise AssertionError(col)

            def prep_load(i):
                j = i % 2
                kb.dma(sp, XT[j][:], xv[:, :, i * NT:(i + 1) * NT], reads=[self.b_x[i]], writes=[bXT[j]])

            def prep(i):
                j = i % 2
                kb.op(act, lambda: nc.scalar.activation(out=SQ[j][:], in_=XT[j][:], func=AF.Square),
                      reads=[bXT[j]], writes=[bSQ[j]])
                for kc in range(KC):
                    kb.op(pe, lambda kc=kc: nc.tensor.matmul(PSS[:], lhsT=ONESB[:], rhs=SQ[j][:, kc, :],
                                                             start=(kc == 0), stop=(kc == KC - 1)),
                          reads=[bONES, bSQ[j]], writes=[bPSS])
                kb.op(dve, lambda: nc.vector.tensor_scalar(out=RS[j][:], in0=PSS[:], scalar1=1.0 / D, scalar2=EPS,
                                                           op0=ALU.mult, op1=ALU.add), reads=[bPSS], writes=[bRS[j]])
                kb.op(act, lambda: nc.scalar.activation(out=RS[j][:], in_=RS[j][:], func=AF.Sqrt),
                      reads=[bRS[j]], writes=[bRS[j]])
                kb.op(dve, lambda: nc.vector.reciprocal(out=RS[j][:], in_=RS[j][:]), reads=[bRS[j]], writes=[bRS[j]])
                for kc in range(KC):
                    kb.op(dve, lambda kc=kc: nc.vector.scalar_tensor_tensor(
                        out=XS[j][:, kc, :], in0=XT[j][:, kc, :], scalar=SM[:, kc:kc + 1], in1=RS[j][:],
                        op0=ALU.mult, op1=ALU.mult), reads=[bXT[j], bSM, bRS[j]], writes=[bXS[j]])

            cnt = {"ps": 0, "ev": 0, "ef": 0, "sg": 0, "evv": 0}

            def nxt(key, n):
                v = cnt[key] % n
                cnt[key] += 1
                return v

            def mm_chunk(j, c, w=128):
                p = nxt("ps", len(PS))
                for kc in range(KC):
                    kb.op(pe, lambda kc=kc: nc.tensor.matmul(PS[p][0:w, :], lhsT=WIN[:, kc, 128 * c:128 * c + w],
                                                             rhs=XS[j][:, kc, :], start=(kc == 0), stop=(kc == KC - 1)),
                          reads=[bwin(128 * c), bXS[j]], writes=[bPS[p]])
                return p

            def store_bf(i, p, dst, scale=None, eng_sel=0):
                e = nxt("ev", len(EV))
                if eng_sel == 0:
                    if scale is None:
                        kb.op(act, lambda: nc.scalar.copy(out=EV[e][:], in_=PS[p][:]), reads=[bPS[p]], writes=[bEV[e]])
                    else:
                        kb.op(act, lambda: nc.scalar.mul(out=EV[e][:], in_=PS[p][:], mul=scale),
                              reads=[bPS[p]], writes=[bEV[e]])
                else:
                    if scale is None:
                        kb.op(dve, lambda: nc.vector.tensor_copy(out=EV[e][:], in_=PS[p][:]),
                              reads=[bPS[p]], writes=[bEV[e]])
                    else:
                        kb.op(dve, lambda: nc.vector.tensor_scalar(out=EV[e][:], in0=PS[p][:], scalar1=scale,
                                                                   scalar2=None, op0=ALU.mult),
                              reads=[bPS[p]], writes=[bEV[e]])
                kb.dma(sp, dst[:, i * NT:(i + 1) * NT], EV[e][:], reads=[bEV[e]], writes=[self.b_a[i]])

            def main(i, mid=None):
                j = i % 2
                k2 = i % 2
                sl = slice(i * NT, (i + 1) * NT)
                for c in range(2):
                    pu = mm_chunk(j, c)
                    pv = mm_chunk(j, c + 2)
                    s_ = nxt("sg", 2)
                    kb.op(act, lambda s_=s_, pv=pv: nc.scalar.activation(out=SG[s_][:], in_=PS[pv][:], func=AF.Sigmoid),
                          reads=[bPS[pv]], writes=[bSG[s_]])
                    if i > 0:
                        kb.op(act, lambda c=c: nc.scalar.copy(out=YPT[c][k2][:, 0:CW - 1],
                                                              in_=YPT[c][1 - k2][:, NT:NT + CW - 1]),
                              reads=[bYPT[c][1 - k2]], writes=[bYPT[c][k2]])
                    kb.op(dve, lambda c=c, pu=pu, s_=s_: nc.vector.tensor_tensor(out=YPT[c][k2][:, CW - 1:], in0=PS[pu][:],
                                                                                 in1=SG[s_][:], op=ALU.mult),
                          reads=[bPS[pu], bSG[s_]], writes=[bYPT[c][k2]])
                for tj in range(CW):
                    for c in range(2):
                        w0 = 16 + 31 * c
                        if tj == 0:
                            kb.op(dve, lambda c=c, w0=w0: nc.vector.tensor_scalar(
                                out=CACC[c][:], in0=YPT[c][k2][:, 0:NT], scalar1=SM[:, w0:w0 + 1],
                                scalar2=SM[:, 78 + c:79 + c], op0=ALU.mult, op1=ALU.add),
                                reads=[bYPT[c][k2], bSM], writes=[bCACC[c]])
                        else:
                            kb.op(dve, lambda c=c, tj=tj, w0=w0: nc.vector.scalar_tensor_tensor(
                                out=CACC[c][:], in0=YPT[c][k2][:, tj:tj + NT], scalar=SM[:, w0 + tj:w0 + tj + 1],
                                in1=CACC[c][:], op0=ALU.mult, op1=ALU.add),
                                reads=[bYPT[c][k2], bSM, bCACC[c]], writes=[bCACC[c]])
                for c in range(4, 10):
                    p = mm_chunk(j, c)
                    store_bf(i, p, self.QT[128 * (c - 4):128 * (c - 3), :], scale=0.125, eng_sel=0)
                if mid is not None:
                    mid()
                for c, dst in ((10, self.KCRT), (11, self.KCRT), (12, self.VCRT), (13, self.VCRT),
                               (14, self.KST), (15, self.KST), (18, self.KWT), (19, self.KWT)):
                    p = mm_chunk(j, c)
                    r0 = 128 * (c % 2)
                    store_bf(i, p, dst[r0:r0 + 128, :], eng_sel=0)
                p = mm_chunk(j, 22, w=36)
                f = nxt("ef", 2)
                kb.op(act, lambda: nc.scalar.activation(out=EF[f][0:36, :], in_=PS[p][0:36, :], func=AF.Sigmoid),
                      reads=[bPS[p]], writes=[bEF[f]])
                kb.dma(sp, self.GT[:, i * NT:(i + 1) * NT], EF[f][0:36, :], reads=[bEF[f]], writes=[self.b_a[i]])
                for s4 in range(4):
                    p = nxt("ps", len(PS))
                    for half, c0 in ((0, 2048), (1, 2560)):
                        for kc in range(KC):
                            kb.op(pe, lambda kc=kc, half=half, c0=c0: nc.tensor.matmul(
                                PS[p][:, 256 * half:256 * (half + 1)], lhsT=XS[j][:, kc, 128 * s4:128 * (s4 + 1)],
                                rhs=WIN[:, kc, c0:c0 + 256], start=(kc == 0), stop=(kc == KC - 1)),
                                reads=[bwin(c0), bXS[j]], writes=[bPS[p]])
                    e = nxt("evv", 2)
                    kb.op(act, lambda: nc.scalar.copy(out=EVV[e][:], in_=PS[p][:]), reads=[bPS[p]], writes=[bEVV[e]])
                    t0 = i * NT + 128 * s4
                    kb.dma(sp, self.VSd[t0:t0 + 128, :], EVV[e][:, 0:256], reads=[bEVV[e]], writes=[self.b_a[i]])
                    kb.dma(sp, self.VWd[t0:t0 + 128, :], EVV[e][:, 256:512], reads=[bEVV[e]], writes=[self.b_a[i]])
                for c in range(2):
                    kb.op(act, lambda c=c: nc.scalar.copy(out=CAB[c][:], in_=CACC[c][:]), reads=[bCACC[c]], writes=[bCAB[c]])
                    kb.op(act, lambda c=c: nc.scalar.activation(out=CSQ[c][:], in_=CACC[c][:], func=AF.Square),
                          reads=[bCACC[c]], writes=[bCSQ[c]])
                pm = nxt("ps", len(PS))
                for c in range(2):
                    kb.op(pe, lambda c=c: nc.tensor.matmul(PS[pm][:], lhsT=ONESB[:], rhs=CAB[c][:], start=(c == 0), stop=(c == 1)),
                          reads=[bONES, bCAB[c]], writes=[bPS[pm]])
                pv2 = nxt("ps", len(PS))
                for c in range(2):
                    kb.op(pe, lambda c=c: nc.tensor.matmul(PS[pv2][:], lhsT=ONESB[:], rhs=CSQ[c][:], start=(c == 0), stop=(c == 1)),
                          reads=[bONES, bCSQ[c]], writes=[bPS[pv2]])
                kb.op(dve, lambda: nc.vector.tensor_scalar(out=CMEAN[:], in0=PS[pm][:], scalar1=1.0 / CC, scalar2=None,
                                                           op0=ALU.mult), reads=[bPS[pm]], writes=[bCMEAN])
                kb.op(pool, lambda: nc.gpsimd.tensor_tensor(out=CVAR[:], in0=CMEAN[:], in1=CMEAN[:], op=ALU.mult),
                      reads=[bCMEAN], writes=[bCVAR])
                kb.op(dve, lambda: nc.vector.scalar_tensor_tensor(out=CVAR[:], in0=PS[pv2][:], scalar=1.0 / CC, in1=CVAR[:],
                                                                  op0=ALU.mult, op1=ALU.subtract),
                      reads=[bPS[pv2], bCVAR], writes=[bCVAR])
                kb.op(act, lambda: nc.scalar.activation(out=CVAR[:], in_=CVAR[:], func=AF.Sqrt, bias=EPSC[:, 0:1]),
                      reads=[bCVAR, bONES], writes=[bCVAR])
                kb.op(dve, lambda: nc.vector.reciprocal(out=CVAR[:], in_=CVAR[:]), reads=[bCVAR], writes=[bCVAR])
                for c in range(2):
                    kb.op(pool, lambda c=c: nc.gpsimd.tensor_tensor(out=CDT[c][:], in0=CACC[c][:], in1=CMEAN[:],
                                                                    op=ALU.subtract),
                          reads=[bCACC[c], bCMEAN], writes=[bCDT[c]])
                    kb.op(pool, lambda c=c: nc.gpsimd.tensor_tensor(out=CDT[c][:], in0=CDT[c][:], in1=CVAR[:], op=ALU.mult),
                          reads=[bCDT[c], bCVAR], writes=[bCDT[c]])
                    kb.op(act, lambda c=c: nc.scalar.activation(out=COB[c][:], in_=CDT[c][:], func=AF.Silu,
                                                                bias=SM[:, 82 + c:83 + c], scale=SM[:, 80 + c:81 + c]),
                          reads=[bCDT[c], bSM], writes=[bCOB[c]])
                    kb.dma(sp, self.MIXT[128 * c:128 * (c + 1), sl], COB[c][:], reads=[bCOB[c]], writes=[self.b_mix[i]])

            prep_load(0)
            prep(0)
            prep_load(1)
            for i in range(NTT):
                def mid(i=i):
                    prep(i + 1)
                    if i + 2 < NTT:
                        prep_load(i + 2)
                main(i, mid=(mid if i + 1 < NTT else None))


    def phase_c(self, l, last):
        kb, nc = self.kb, self.nc
        with Ph(kb) as pho:
            WGU = pho.sb("c2_wgu", [128, KC, 2 * DFF], BF16)
            WD = pho.sb("c2_wd", [128, FC, D], BF16)
            wts = (WGU, WD, Buf(), Buf())
            self.phase_c1(l, wts)
            self.phase_c2(l, last, wts)

    def phase_c1(self, l, wts):
        kb, nc = self.kb, self.nc
        pe, act, dve, pool, sp = kb.pe, kb.act, kb.dve, kb.pool, kb.sp
        xsrc = self.xin if l == 0 else self.xres
        xv = xsrc.rearrange("(kc p) t -> p kc t", p=128)
        xo = self.xres.rearrange("(kc p) t -> p kc t", p=128)
        mv = self.MIXT.rearrange("(kc p) t -> p kc t", p=128)
        with Ph(kb) as ph:
            WO = ph.sb("c1_wo", [128, KC, D], BF16)
            XT = [ph.sb(f"c1_xt{i}", [128, KC, NT]) for i in range(2)]
            MX = [ph.sb(f"c1_mx{i}", [128, KC, NT], BF16) for i in range(2)]
            PS = [ph.ps(f"c1_ps{i}", [128, NT]) for i in range(4)]
            bWO = Buf(); bXT = [Buf(), Buf()]; bMX = [Buf(), Buf()]; bPS = [Buf() for _ in PS]
            wv = self.w_out[l].rearrange("(kc p) n -> p kc n", p=128)
            for kc in range(KC):
                kb.dma(pool, WO[:, kc, :], wv[:, kc, :], writes=[bWO])
            WGU, WD, bW, bWD = wts
            gv = self.w_gu[l].rearrange("(kc p) n -> p kc n", p=128)
            for kc in range(KC):
                for hf in range(2):
                    kb.dma(pool, WGU[:, kc, DFF * hf:DFF * (hf + 1)], gv[:, kc, DFF * hf:DFF * (hf + 1)], writes=[bW])
            dv = self.w_dn[l].rearrange("(f p) n -> p f n", p=128)
            for f0 in range(0, FC, 6):
                f1 = min(FC, f0 + 6)
                kb.dma(pool, WD[:, f0:f1, :], dv[:, f0:f1, :], writes=[bWD])

            def load(i):
                j = i % 2
                kb.dma(sp, XT[j][:], xv[:, :, i * NT:(i + 1) * NT], reads=[self.b_x[i]], writes=[bXT[j]])
                kb.dma(act, MX[j][:], mv[:, :, i * NT:(i + 1) * NT], reads=[self.b_mix[i]], writes=[bMX[j]])

            load(0)
            pc = 0
            for i in range(NTT):
                j = i % 2
                if i + 1 < NTT:
                    load(i + 1)
                for n in range(KC):
                    p = pc % len(PS); pc += 1
                    for kc in range(KC):
                        kb.op(pe, lambda kc=kc, n=n, p=p: nc.tensor.matmul(
                            PS[p][:], lhsT=WO[:, kc, 128 * n:128 * (n + 1)], rhs=MX[j][:, kc, :],
                            start=(kc == 0), stop=(kc == KC - 1)), reads=[bWO, bMX[j]], writes=[bPS[p]])
                    kb.op(dve, lambda n=n, p=p: nc.vector.tensor_tensor(out=XT[j][:, n, :], in0=XT[j][:, n, :],
                                                                          in1=PS[p][:], op=ALU.add),
                          reads=[bPS[p], bXT[j]], writes=[bXT[j]])
                kb.dma(sp, xo[:, :, i * NT:(i + 1) * NT], XT[j][:], reads=[bXT[j]], writes=[self.b_x[i]])

    def phase_c2(self, l, last, wts):
        kb, nc = self.kb, self.nc
        WGU, WD, bW, bWD = wts
        pe, act, dve, pool, sp = kb.pe, kb.act, kb.dve, kb.pool, kb.sp
        xv = self.xres.rearrange("(kc p) t -> p kc t", p=128)
        ov = self.outT.rearrange("(kc p) t -> p kc t", p=128)
        with Ph(kb) as ph:
            SM = ph.sb("c2_sm", [128, NSM])
            FIN = ph.sb("c2_fin", [128, 8])
            ONESB = ph.sb("c2_ones", [128, 128], BF16)
            XT = [ph.sb(f"c2_xt{i}", [128, KC, NT]) for i in range(2)]
            XS = [ph.sb(f"c2_xs{i}", [128, KC, NT], BF16) for i in range(2)]
            RS0 = ph.sb("c2_rs", [128, NT])
            RS = [RS0, RS0]
            ACTT = ph.sb("c2_actt", [128, FC, NT], BF16)
            SQ = ph.sb("c2_sq", [128, 1, NT], BF16)
            bSQ = Buf()
            SG0 = ph.sb("c2_sg", [128, NT])
            SG = [SG0, SG0]
            PS = [ph.ps(f"c2_ps{i}", [128, NT]) for i in range(7)]
            PSS = ph.ps("c2_pss", [128, NT])
            bSM, bONES, bPSS, bACTT = Buf(), Buf(), Buf(), Buf()
            bXT = [Buf(), Buf()]; bXS = [Buf(), Buf()]
            bRS0 = Buf(); bRS = [bRS0, bRS0]; bSG0 = Buf(); bSG = [bSG0, bSG0]
            bPS = [Buf() for _ in PS]
            kb.dma(sp, SM[:], self.small[:, l, :], writes=[bSM])
            kb.dma(sp, FIN[:], self.fin[:, :], writes=[bSM])
            kb.op(dve, lambda: nc.vector.memset(ONESB[:], 1.0), writes=[bONES])
            def stats(j, gcol, gt):
                for kc in range(KC):
                    kb.op(act, lambda kc=kc: nc.scalar.activation(out=SQ[:, 0, :], in_=XT[j][:, kc, :], func=AF.Square),
                          reads=[bXT[j]], writes=[bSQ])
                    kb.op(pe, lambda kc=kc: nc.tensor.matmul(PSS[:], lhsT=ONESB[:], rhs=SQ[:, 0, :],
                                                             start=(kc == 0), stop=(kc == KC - 1)),
                          reads=[bONES, bSQ], writes=[bPSS])
                kb.op(dve, lambda: nc.vector.tensor_scalar(out=RS[j][:], in0=PSS[:], scalar1=1.0 / D, scalar2=EPS,
                                                           op0=ALU.mult, op1=ALU.add), reads=[bPSS], writes=[bRS[j]])
                kb.op(act, lambda: nc.scalar.activation(out=RS[j][:], in_=RS[j][:], func=AF.Sqrt),
                      reads=[bRS[j]], writes=[bRS[j]])
                kb.op(dve, lambda: nc.vector.reciprocal(out=RS[j][:], in_=RS[j][:]), reads=[bRS[j]], writes=[bRS[j]])

            def prep_load(i):
                j = i % 2
                kb.dma(sp, XT[j][:], xv[:, :, i * NT:(i + 1) * NT], reads=[self.b_x[i]], writes=[bXT[j]])

            def prep(i):
                j = i % 2
                stats(j, 8, SM)
                for kc in range(KC):
                    kb.op(dve, lambda kc=kc: nc.vector.scalar_tensor_tensor(
                        out=XS[j][:, kc, :], in0=XT[j][:, kc, :], scalar=SM[:, 8 + kc:9 + kc], in1=RS[j][:],
                        op0=ALU.mult, op1=ALU.mult), reads=[bXT[j], bSM, bRS[j]], writes=[bXS[j]])

            pc = [0]

            def nps():
                p = pc[0] % len(PS)
                pc[0] += 1
                return p

            def main(i, mid=None):
                j = i % 2
                if i + 1 < NTT:
                    prep_load(i + 1)
                for f in range(FC):
                    if f == 8 and mid is not None:
                        mid()
                    pg_ = nps()
                    for kc in range(KC):
                        kb.op(pe, lambda kc=kc, f=f, p=pg_: nc.tensor.matmul(
                            PS[p][:], lhsT=WGU[:, kc, 128 * f:128 * (f + 1)], rhs=XS[j][:, kc, :],
                            start=(kc == 0), stop=(kc == KC - 1)), reads=[bW, bXS[j]], writes=[bPS[pg_]])
                    pu_ = nps()
                    for kc in range(KC):
                        kb.op(pe, lambda kc=kc, f=f, p=pu_: nc.tensor.matmul(
                            PS[p][:], lhsT=WGU[:, kc, DFF + 128 * f:DFF + 128 * (f + 1)], rhs=XS[j][:, kc, :],
                            start=(kc == 0), stop=(kc == KC - 1)), reads=[bW, bXS[j]], writes=[bPS[pu_]])
                    s = f % 2
                    kb.op(act, lambda s=s, p=pg_: nc.scalar.activation(out=SG[s][:], in_=PS[p][:], func=AF.Silu),
                          reads=[bPS[pg_]], writes=[bSG[s]])
                    kb.op(dve, lambda s=s, p=pu_, f=f: nc.vector.tensor_tensor(out=ACTT[:, f, :], in0=PS[p][:],
                                                                               in1=SG[s][:], op=ALU.mult),
                          reads=[bPS[pu_], bSG[s]], writes=[bACTT])
                for n in range(KC):
                    p = nps()
                    for f in range(FC):
                        kb.op(pe, lambda f=f, n=n, p=p: nc.tensor.matmul(
                            PS[p][:], lhsT=WD[:, f, 128 * n:128 * (n + 1)], rhs=ACTT[:, f, :],
                            start=(f == 0), stop=(f == FC - 1)), reads=[bWD, bACTT], writes=[bPS[p]])
                    kb.op(dve, lambda n=n, p=p: nc.vector.tensor_tensor(out=XT[j][:, n, :], in0=XT[j][:, n, :],
                                                                          in1=PS[p][:], op=ALU.add),
                          reads=[bPS[p], bXT[j]], writes=[bXT[j]])
                if not last:
                    kb.dma(sp, xv[:, :, i * NT:(i + 1) * NT], XT[j][:], reads=[bXT[j]], writes=[self.b_x[i]])
                else:
                    stats(j, 0, FIN)
                    for kc in range(KC):
                        kb.op(dve, lambda kc=kc: nc.vector.scalar_tensor_tensor(
                            out=XT[j][:, kc, :], in0=XT[j][:, kc, :], scalar=FIN[:, kc:kc + 1], in1=RS[j][:],
                            op0=ALU.mult, op1=ALU.mult), reads=[bXT[j], bSM, bRS[j]], writes=[bXT[j]])
                    kb.dma(sp, ov[:, :, i * NT:(i + 1) * NT], XT[j][:], reads=[bXT[j]], writes=[self.b_x[i]])

            prep_load(0)
            prep(0)
            for i in range(NTT):
                main(i, mid=((lambda i=i: prep(i + 1)) if i + 1 < NTT else None))


    def phase_b0(self, l):
        kb, nc = self.kb, self.nc
        pe, act, dve, pool, sp = kb.pe, kb.act, kb.dve, kb.pool, kb.sp
        with Ph(kb) as ph:
            SM = ph.sb("b0_sm", [128, NSM])
            ONESB = ph.sb("b0_ones", [128, 128], BF16)
            YP = ph.sb("b0_yp", [128, 2, CW - 1 + T])
            ACC = ph.sb("b0_acc", [128, 2, T])
            AB = ph.sb("b0_ab", [128, 2, T], BF16)
            SQB = ph.sb("b0_sqb", [128, 2, T], BF16)
            MEAN = [ph.sb(f"b0_mean{i}", [128, NT]) for i in range(2)]
            VAR = [ph.sb(f"b0_var{i}", [128, NT]) for i in range(2)]
            DT = [ph.sb(f"b0_dt{i}", [128, NT]) for i in range(2)]
            OB = [ph.sb(f"b0_ob{i}", [128, NT], BF16) for i in range(2)]
            PM = [ph.ps(f"b0_pm{i}", [128, NT]) for i in range(2)]
            PV = [ph.ps(f"b0_pv{i}", [128, NT]) for i in range(2)]
            bSM, bONES = Buf(), Buf()
            bYP = [Buf(), Buf()]; bACC = [Buf(), Buf()]; bAB = [Buf(), Buf()]; bSQB = [Buf(), Buf()]
            bMEAN = [Buf(), Buf()]; bVAR = [Buf(), Buf()]; bDT = [Buf(), Buf()]; bOB = [Buf(), Buf()]
            bPM = [Buf(), Buf()]; bPV = [Buf(), Buf()]
            kb.dma(sp, SM[:], self.small[:, l, :], writes=[bSM])
            kb.op(dve, lambda: nc.vector.memset(ONESB[:], 1.0), writes=[bONES])
            for c in range(2):
                kb.op(dve, lambda c=c: nc.vector.memset(YP[:, c, 0:CW - 1], 0.0), writes=[bYP[c]])
                kb.dma(sp, YP[:, c, CW - 1:], self.YT[128 * c:128 * (c + 1), :], writes=[bYP[c]])
            for j in range(CW):
                for c in range(2):
                    w0 = 16 + 31 * c
                    eng, eh = dve, nc.vector
                    if j == 0:
                        kb.op(eng, lambda c=c, w0=w0, eh=eh: eh.tensor_scalar(
                            out=ACC[:, c, :], in0=YP[:, c, 0:T], scalar1=SM[:, w0:w0 + 1], scalar2=SM[:, 78 + c:79 + c],
                            op0=ALU.mult, op1=ALU.add), reads=[bYP[c], bSM], writes=[bACC[c]])
                    else:
                        kb.op(eng, lambda c=c, j=j, w0=w0, eh=eh: eh.scalar_tensor_tensor(
                            out=ACC[:, c, :], in0=YP[:, c, j:j + T], scalar=SM[:, w0 + j:w0 + j + 1], in1=ACC[:, c, :],
                            op0=ALU.mult, op1=ALU.add), reads=[bYP[c], bSM, bACC[c]], writes=[bACC[c]])
            for c in range(2):
                kb.op(act, lambda c=c: nc.scalar.copy(out=AB[:, c, :], in_=ACC[:, c, :]), reads=[bACC[c]], writes=[bAB[c]])
                kb.op(act, lambda c=c: nc.scalar.activation(out=SQB[:, c, :], in_=ACC[:, c, :], func=AF.Square),
                      reads=[bACC[c]], writes=[bSQB[c]])
            for i in range(NTT):
                j = i % 2
                sl = slice(i * NT, (i + 1) * NT)
                for c in range(2):
                    kb.op(pe, lambda c=c: nc.tensor.matmul(PM[j][:], lhsT=ONESB[:], rhs=AB[:, c, sl],
                                                           start=(c == 0), stop=(c == 1)),
                          reads=[bONES, bAB[c]], writes=[bPM[j]])
                for c in range(2):
                    kb.op(pe, lambda c=c: nc.tensor.matmul(PV[j][:], lhsT=ONESB[:], rhs=SQB[:, c, sl],
                                                           start=(c == 0), stop=(c == 1)),
                          reads=[bONES, bSQB[c]], writes=[bPV[j]])
                kb.op(dve, lambda: nc.vector.tensor_scalar(out=MEAN[j][:], in0=PM[j][:], scalar1=1.0 / CC, scalar2=None,
                                                           op0=ALU.mult), reads=[bPM[j]], writes=[bMEAN[j]])
                kb.op(dve, lambda: nc.vector.tensor_tensor(out=VAR[j][:], in0=MEAN[j][:], in1=MEAN[j][:], op=ALU.mult),
                      reads=[bMEAN[j]], writes=[bVAR[j]])
                kb.op(dve, lambda: nc.vector.scalar_tensor_tensor(out=VAR[j][:], in0=PV[j][:], scalar=1.0 / CC,
                                                                  in1=VAR[j][:], op0=ALU.mult, op1=ALU.subtract),
                      reads=[bPV[j], bVAR[j]], writes=[bVAR[j]])
                kb.op(dve, lambda: nc.vector.tensor_scalar(out=VAR[j][:], in0=VAR[j][:], scalar1=EPS, scalar2=None,
                                                           op0=ALU.add), reads=[bVAR[j]], writes=[bVAR[j]])
                kb.op(act, lambda: nc.scalar.activation(out=VAR[j][:], in_=VAR[j][:], func=AF.Sqrt),
                      reads=[bVAR[j]], writes=[bVAR[j]])
                kb.op(dve, lambda: nc.vector.reciprocal(out=VAR[j][:], in_=VAR[j][:]), reads=[bVAR[j]], writes=[bVAR[j]])
                for c in range(2):
                    kb.op(dve, lambda c=c: nc.vector.tensor_tensor(out=DT[c][:], in0=ACC[:, c, sl], in1=MEAN[j][:],
                                                                   op=ALU.subtract),
                          reads=[bACC[c], bMEAN[j]], writes=[bDT[c]])
                    kb.op(dve, lambda c=c: nc.vector.tensor_tensor(out=DT[c][:], in0=DT[c][:], in1=VAR[j][:], op=ALU.mult),
                          reads=[bDT[c], bVAR[j]], writes=[bDT[c]])
                    kb.op(act, lambda c=c: nc.scalar.activation(out=OB[c][:], in_=DT[c][:], func=AF.Silu,
                                                                bias=SM[:, 82 + c:83 + c], scale=SM[:, 80 + c:81 + c]),
                          reads=[bDT[c], bSM], writes=[bOB[c]])
                    kb.dma(pool, self.MIXT[128 * c:128 * (c + 1), sl], OB[c][:], reads=[bOB[c]], writes=[self.b_mix[i]])

    def phase_b12(self, l, groups=range(NKV), qtiles=range(NTT)):
        kb, nc = self.kb, self.nc
        pe, act, dve, pool, sp = kb.pe, kb.act, kb.dve, kb.pool, kb.sp
        with Ph(kb) as ph:
            MASK = ph.sb("b_mask", [128, 13, NT], BF16)
            BIAS = ph.sb("b_bias", [128, 612])
            IOTA = ph.sb("b_iota", [128, NT])
            OV = ph.sb("b_ov", [128, 2, 65], BF16)
            FB = ph.sb("b_fb", [128, 32, 64])
            IDENT = ph.sb("b_ident", [128, 128], BF16)
            IDENTBIG = ph.sb("b_identbig", [128, 128], BF16)
            ONESF = ph.sb("b_onesf", [128, 64])
            KCA = ph.sb("b_kca", [128, NKV, 256], BF16)
            VCA = ph.sb("b_vca", [128, NKV, 2, 128], BF16)
            PT = [ph.sb(f"b_pt{i}", [128, NT], BF16) for i in range(3)]
            RB = [ph.sb(f"b_rb{i}", [64, NT]) for i in range(3)]
            EPS30 = ph.sb("b_eps30", [128, 1])
            GR = [ph.sb(f"b_gr{i}", [64, 9, NT]) for i in range(2)]
            NACC = ph.sb("b_nacc", [64, RPG, NT])
            TMP = [ph.sb(f"b_tmp{i}", [64, NT]) for i in range(3)]
            NOUT = [ph.sb(f"b_nout{i}", [64, NT], BF16) for i in range(2)]
            IMP = ph.sb("b_imp", [128, 4, 64])
            M8 = ph.sb("b_m8", [128, 4, 8])
            NM = ph.sb("b_nm", [128, 4, 128], BF16)
            RSC = ph.sb("b_rsc", [128, 4])
            PSs = [ph.ps(f"b_pss{i}", [128, NT]) for i in range(4)]
            PSo = [ph.ps(f"b_pso{i}", [128, NT]) for i in range(3)]
            PSX = ph.ps("b_psx", [128, NT])

            bC = Buf()
            bW1, bW2, bPET, bCB, bKCR = Buf(), Buf(), Buf(), Buf(), Buf()
            bHS = [Buf(), Buf()]
            bKCA, bVCA = Buf(), Buf()
            bKAlo = [Buf(), Buf()]; bKAhi = [Buf(), Buf()]; bKWlo = [Buf(), Buf()]; bKWhi = [Buf(), Buf()]
            bVS = [Buf(), Buf()]; bVW = [Buf(), Buf()]; bVone = [Buf(), Buf()]
            bQlo = [[Buf() for _ in range(RPG)] for _ in range(2)]
            bQhi = [[[Buf() for _ in range(NTT)] for _ in range(RPG)] for _ in range(2)]
            bPT = [Buf() for _ in PT]; bRB = [Buf() for _ in range(3)]
            bGR = [Buf(), Buf()]; bNACC = [Buf() for _ in range(RPG)]; bTMP = [Buf(), Buf(), Buf()]; bNOUT = [Buf(), Buf()]
            bIMP = [Buf() for _ in range(4)]; bM8 = [Buf() for _ in range(4)]; bNM = [Buf() for _ in range(4)]
            bRSC = Buf()
            bPSs = [Buf() for _ in range(4)]; bPSo = [Buf() for _ in range(3)]; bPSX = Buf()

            cst = self.cst

            def load_constants():
                for v0 in range(0, 13, 4):
                    v1 = min(13, v0 + 4)
                    kb.dma(pool, MASK[:, v0:v1, :], cst["c_mask"][:, v0:v1, :], writes=[bC])
                kb.dma(sp, BIAS[:], cst["c_bias"][:, :], writes=[bC])
                kb.dma(sp, IOTA[:], cst["c_iota"][:, :], writes=[bC])
                kb.dma(pool, OV[:], cst["c_ov"][:, :, :], writes=[bC])
                kb.dma(sp, FB[:], cst["c_fb"][:, :, :], writes=[bC])
                kb.dma(pool, IDENT[:], cst["c_ident"][:, :], writes=[bC])
                kb.dma(pool, IDENTBIG[:], cst["c_identbig"][:, :], writes=[bC])
            kb.op(dve, lambda: nc.vector.memset(ONESF[:], 1.0), writes=[bC])
            kb.op(dve, lambda: nc.vector.memset(EPS30[:], 1e-30), writes=[bC])
            kb.op(dve, lambda: nc.vector.memset(KCA[:], 0.0), writes=[bKCA])
            kb.op(dve, lambda: nc.vector.memset(KCA[64:65, :, :], 1.0), writes=[bKCA])
            kb.op(dve, lambda: nc.vector.memset(VCA[:], 0.0), writes=[bVCA])
            kb.op(dve, lambda: nc.vector.memset(VCA[:, :, :, 64:128], 1.0), writes=[bVCA])
            kb.op(dve, lambda: nc.vector.memset(NM[:], 0.0), writes=bNM)

            pss_i = [0]

            def npss():
                p = pss_i[0] % 4
                pss_i[0] += 1
                return p

            with Ph(kb) as ph1:
                W1s = [ph1.sb(f"b_w1_{i}", [128, 32, 256], BF16) for i in range(2)]
                W2 = ph1.sb("b_w2", [128, 2, 2, 64], BF16)
                PET = ph1.sb("b_pet", [128, 2, 32], BF16)
                CB = ph1.sb("b_cb", [128, 2, 2])
                KCRs = [ph1.sb(f"b_kcr{i}", [128, 2, T], BF16) for i in range(2)]
                KCR2 = ph1.sb("b_kcr2", [128, 2, 16, 256], BF16)
                HS = [ph1.sb(f"b_hs{i}", [128, 2, 256], BF16) for i in range(2)]
                bW1s = [[[Buf() for _ in range(4)] for _ in range(2)] for _ in range(2)]
                bKCRs = [Buf(), Buf()]; bKCR2 = Buf()
                for kv in range(2):
                    w1 = (self.w1k, self.w1v)[kv][l].rearrange("l d h -> d l h")
                    w2 = (self.w2k, self.w2v)[kv][l].rearrange("(hc p) d -> p hc d", p=128)
                    src = (self.KCRT, self.VCRT)[kv]
                    for half in range(2):
                        for l0 in range(0, 32, 8):
                            kb.dma(pool, W1s[kv][64 * half:64 * (half + 1), l0:l0 + 8, :], w1[:, l0:l0 + 8, :],
                                   writes=[bW1s[kv][half][l0 // 8]])
                    kb.dma(pool, W2[:, kv, :, :], w2, writes=[bW2])
                    kb.dma(pool, PET[:, kv, :], self.pe_t[:, l, kv, :], writes=[bPET])
                    kb.dma(sp, KCRs[kv][:], src.rearrange("(c p) t -> p c t", p=128), writes=[bKCRs[kv]])
                load_constants()
                for hb in range(2):
                    kb.op(dve, lambda hb=hb: nc.vector.memset(HS[hb][:], 0.0), writes=[bHS[hb]])
                for kv in range(2):
                    W1 = W1s[kv]
                    bW1 = bW1s[kv]
                    for c in range(2):
                        eng, eh = (act, nc.scalar) if c == 0 else (pool, nc.gpsimd)
                        if c == 0:
                            kb.op(act, lambda c=c: nc.scalar.copy(
                                out=KCR2[:, c, :, :], in_=KCRs[kv][:, c, :].rearrange("p (n s) -> p s n", s=16)),
                                reads=[bKCRs[kv]], writes=[bKCR2])
                        else:
                            kb.op(dve, lambda c=c: nc.vector.tensor_copy(
                                out=KCR2[:, c, :, :], in_=KCRs[kv][:, c, :].rearrange("p (n s) -> p s n", s=16)),
                                reads=[bKCRs[kv]], writes=[bKCR2])
                    for hc in range(2):
                        p = npss()
                        for ll in range(32):
                            kb.op(pe, lambda ll=ll, hc=hc, p=p: nc.tensor.matmul(
                                PSs[p][:, 0:1], lhsT=W1[0:64, ll, 128 * hc:128 * (hc + 1)], rhs=PET[0:64, kv, ll:ll + 1],
                                start=(ll == 0), stop=(ll == 31)), reads=[bW1[0][ll // 8], bPET], writes=[bPSs[p]])
                        kb.op(dve, lambda hc=hc, p=p: nc.vector.tensor_copy(out=CB[:, kv, hc:hc + 1], in_=PSs[p][:, 0:1]),
                              reads=[bPSs[p]], writes=[bCB])
                    for g in range(NKV):
                        pb = 64 * (g % 2)
                        ch = g // 2
                        hb = g % 2
                        for hc in range(2):
                            p = npss()
                            for ll in range(32):
                                n0 = 0 if ll < 16 else 1
                                kb.op(pe, lambda ll=ll, hc=hc, p=p, n0=n0: nc.tensor.matmul(
                                    PSs[p][:, 0:255], lhsT=W1[pb:pb + 64, ll, 128 * hc:128 * (hc + 1)],
                                    rhs=KCR2[pb:pb + 64, ch, ll % 16, n0:n0 + 255],
                                    start=(ll == 0), stop=(ll == 31)), reads=[bW1[g % 2][ll // 8], bKCR2], writes=[bPSs[p]])
                            kb.op(act, lambda hc=hc, p=p: nc.scalar.activation(
                                out=HS[hb][:, hc, 0:255], in_=PSs[p][:, 0:255], func=AF.Silu, bias=CB[:, kv, hc:hc + 1]),
                                reads=[bPSs[p], bCB], writes=[bHS[hb]])
                        if kv == 0:
                            p = npss()
                            for hc in range(2):
                                kb.op(pe, lambda hc=hc, p=p: nc.tensor.matmul(
                                    PSs[p][0:64, 0:255], lhsT=W2[:, 0, hc, :], rhs=HS[hb][:, hc, 0:255],
                                    start=(hc == 0), stop=(hc == 1)), reads=[bW2, bHS[hb]], writes=[bPSs[p]])
                            kb.op(dve, lambda p=p, g=g: nc.vector.tensor_copy(out=KCA[0:64, g, 0:255], in_=PSs[p][0:64, 0:255]),
                                  reads=[bPSs[p]], writes=[bKCA])
                        else:
                            for ntile in range(2):
                                p = npss()
                                for hc in range(2):
                                    kb.op(pe, lambda hc=hc, p=p, ntile=ntile: nc.tensor.matmul(
                                        PSs[p][:, 0:64], lhsT=HS[hb][:, hc, 128 * ntile:128 * (ntile + 1)], rhs=W2[:, 1, hc, :],
                                        start=(hc == 0), stop=(hc == 1)), reads=[bW2, bHS[hb]], writes=[bPSs[p]])
                                kb.op(dve, lambda p=p, g=g, ntile=ntile: nc.vector.tensor_copy(
                                    out=VCA[:, g, ntile, 0:64], in_=PSs[p][:, 0:64]), reads=[bPSs[p]], writes=[bVCA])

            KA = [ph.sb(f"b_ka{i}", [128, T], BF16) for i in range(2)]
            KW = [ph.sb(f"b_kw{i}", [128, T], BF16) for i in range(2)]
            VS = [ph.sb(f"b_vs{i}", [128, 32, 128], BF16) for i in range(2)]
            VW = [ph.sb(f"b_vw{i}", [128, 32, 128], BF16) for i in range(2)]
            QA = [ph.sb(f"b_qa{i}", [128, RPG, T], BF16) for i in range(2)]
            for st in range(2):
                kb.dma(pool, KA[st][64:128, :], cst["c_E"][:, :], writes=[bKAhi[st]])
                kb.op(dve, lambda st=st: nc.vector.memset(KW[st][64:65, :], 1.0), writes=[bKWhi[st]])
                kb.op(dve, lambda st=st: nc.vector.memset(VS[st][:, :, 64:128], 1.0), writes=[bVone[st]])
                kb.op(dve, lambda st=st: nc.vector.memset(VW[st][:, :, 64:128], 1.0), writes=[bVone[st]])
            if "KCA" in self.debug:
                kb.dma(sp, self.dbg_kca[:, :, :], KCA[0:64, :, :], reads=[bKCA])
                kb.dma(sp, self.dbg_vca[:, :, :, :], VCA[:, :, :, 0:64], reads=[bVCA])

            ctr = {"pso": 0, "pt": 0, "osb": 0, "bc": 0, "gr": 0, "tmp": 0, "nout": 0}

            def nx(key, n):
                v = ctr[key] % n
                ctr[key] += 1
                return v

            stream = []
            deferred = []

            def add(**kw):
                stream.append(kw)

            def fin_stage1(ob, r, br, gi):
                def stage():
                    o2 = nx("osb", 3)
                    if br == 0:
                        kb.op(act, lambda: nc.scalar.activation(out=RB[o2][:, :], in_=PSo[ob][64:128, :], func=AF.Ln,
                                                                bias=EPS30[64:128, 0:1]), reads=[bPSo[ob], bC], writes=[bRB[o2]])
                        kb.op(act, lambda: nc.scalar.activation(out=RB[o2][:, :], in_=RB[o2][:, :], func=AF.Exp, scale=-1.0),
                              reads=[bRB[o2]], writes=[bRB[o2]])
                    else:
                        kb.op(dve, lambda: nc.vector.reciprocal(out=RB[o2][:, :], in_=PSo[ob][64:128, :]),
                              reads=[bPSo[ob]], writes=[bRB[o2]])
                    t2 = nx("tmp", 3)
                    kb.op(dve, lambda: nc.vector.tensor_tensor(out=TMP[t2][:], in0=PSo[ob][0:64, :], in1=RB[o2][:, :],
                                                               op=ALU.mult), reads=[bPSo[ob], bRB[o2]], writes=[bTMP[t2]])
                    if br == 0:
                        kb.op(pool, lambda: nc.gpsimd.tensor_tensor(out=NACC[:, r, :], in0=TMP[t2][:],
                                                                    in1=GR[gi][0:64, 3 * r + br, :], op=ALU.mult),
                              reads=[bTMP[t2], bGR[gi]], writes=[bNACC[r]])
                    else:
                        kb.op(pool, lambda: nc.gpsimd.tensor_tensor(out=TMP[t2][:], in0=TMP[t2][:],
                                                                    in1=GR[gi][0:64, 3 * r + br, :], op=ALU.mult),
                              reads=[bTMP[t2], bGR[gi]], writes=[bTMP[t2]])
                        kb.op(pool, lambda: nc.gpsimd.tensor_tensor(out=NACC[:, r, :], in0=NACC[:, r, :], in1=TMP[t2][:],
                                                                    op=ALU.add),
                              reads=[bTMP[t2], bNACC[r]], writes=[bNACC[r]])
                return stage

            vsv = self.VSd.rearrange("(kt p) (g d) -> p kt g d", p=128, g=NKV)
            vwv = self.VWd.rearrange("(kt p) (g d) -> p kt g d", p=128, g=NKV)

            def group_loads(g, st):
                def f():
                    kb.dma(sp, KA[st][0:64, :], self.KST[64 * g:64 * (g + 1), :], writes=[bKAlo[st]])
                    kb.dma(sp, KW[st][0:64, :], self.KWT[64 * g:64 * (g + 1), :], writes=[bKWlo[st]])
                    for k0 in range(0, 32, 8):
                        kb.dma(sp, VS[st][:, k0:k0 + 8, 0:64], vsv[:, k0:k0 + 8, g, :], writes=[bVS[st]])
                        kb.dma(sp, VW[st][:, k0:k0 + 8, 0:64], vwv[:, k0:k0 + 8, g, :], writes=[bVW[st]])
                    qv = self.QT[192 * g:192 * (g + 1), :].rearrange("(r d) t -> d r t", r=RPG)
                    for r in range(RPG):
                        kb.dma(sp, QA[st][0:64, r, :], qv[:, r, :], writes=[bQlo[st][r]])
                        h = 3 * g + r
                        kb.dma(sp, QA[st][64:65, r, :], cst["c_cq"][h:h + 1, :], writes=[bQhi[st][r][i] for i in qtiles])
                return f

            def run_stream():
                DEPTH_ = 3
                nb = len(stream)
                _run(DEPTH_, nb)
                stream[:] = []

            def _run(DEPTH_, nb):
              if True:
                def issue_qk(b):
                    blk = stream[b]
                    if blk["pre"] is not None:
                        blk["pre"]()
                    p = npss()
                    blk["p"] = p
                    blk["qk"](p)

                for b in range(min(DEPTH_, nb)):
                    issue_qk(b)
                for b in range(nb):
                    if b + DEPTH_ < nb:
                        issue_qk(b + DEPTH_)
                    blk = stream[b]
                    p = blk["p"]
                    pt = nx("pt", 3)
                    col = blk["col"]
                    e0, e1 = blk.get("cr", (0, NT))
                    kb.op(act, lambda p=p, pt=pt, col=col, e0=e0, e1=e1: nc.scalar.activation(
                        out=PT[pt][:, e0:e1], in_=PSs[p][:, e0:e1], func=AF.Exp, bias=BIAS[:, col:col + 1]),
                        reads=[bPSs[p], bC], writes=[bPT[pt]])
                    blk["pv"](pt)
                    keep = []
                    for due, fn in deferred:
                        if due <= b:
                            fn()
                        else:
                            keep.append((due, fn))
                    deferred[:] = keep
                    if blk["post"] is not None:
                        blk["post"](b)
                for due, fn in deferred:
                    fn()
                deferred[:] = []

            glist = list(groups)
            qlist = list(qtiles)
            for gidx, g in enumerate(glist):
                st = gidx % 2
                for ti, i in enumerate(qlist):
                    qs = slice(i * NT, (i + 1) * NT)
                    gi = nx("gr", 2)
                    kts = [0] if i <= 3 else [0, 1]
                    pf = []
                    if gidx == 0 and ti == 0:
                        pf.append(group_loads(g, st))
                    if gidx + 1 < len(glist) and ti == min(1, len(qlist) - 1):
                        pf.append(group_loads(glist[gidx + 1], 1 - st))

                    def tile_pre(g=g, i=i, gi=gi, qs=qs, pf=pf):
                        for f_ in pf:
                            f_()
                        kb.dma(sp, GR[gi][:, :, :],
                               self.GT[9 * g:9 * (g + 1), qs].rearrange("(o r) t -> o r t", o=1).to_broadcast([64, 9, NT]),
                               writes=[bGR[gi]])
                    first_blk = True

                    for r in range(RPG):
                        h = 3 * g + r
                        ob_box = {}
                        for kt in kts:
                            midx = None
                            if kt == 0 and i <= 4:
                                midx = 8 + i
                            elif kt == 1:
                                midx = 8 + (i - 4)

                            def qk(p, kt=kt, r=r, midx=midx, g=g, qs=qs, i=i, st=st):
                                kb.op(pe, lambda: nc.tensor.matmul(
                                    PSs[p][:], lhsT=KCA[0:65, g, 128 * kt:128 * (kt + 1)], rhs=QA[st][0:65, r, qs],
                                    start=True, stop=(midx is None)), reads=[bKCA, bQlo[st][r], bQhi[st][r][i]],
                                    writes=[bPSs[p]])
                                if midx is not None:
                                    kb.op(pe, lambda: nc.tensor.matmul(
                                        PSs[p][:], lhsT=IDENT[:], rhs=MASK[:, midx, :], start=False, stop=True),
                                        reads=[bC], writes=[bPSs[p]])
                            col = 420 + (h * 2 + kt) * 8 + i

                            def pv(pt, kt=kt, g=g, ob_box=ob_box, first=(kt == kts[0]), last=(kt == kts[-1])):
                                if first:
                                    ob_box["ob"] = nx("pso", 3)
                                ob = ob_box["ob"]
                                kb.op(pe, lambda: nc.tensor.matmul(
                                    PSo[ob][:, :], lhsT=VCA[:, g, kt, :], rhs=PT[pt][:], start=first, stop=last),
                                    reads=[bVCA, bPT[pt]], writes=[bPSo[ob]])
                                ob_box.setdefault("pts", []).append((kt, pt))
                                if last:
                                    pts = ob_box["pts"]
                                    for s4 in range(4):
                                        for n_, (kt_, pt_) in enumerate(pts):
                                            kb.op(pe, lambda s4=s4, kt_=kt_, pt_=pt_, n_=n_: nc.tensor.matmul(
                                                PSX[:, 128 * s4:128 * s4 + 65], lhsT=PT[pt_][:, 128 * s4:128 * (s4 + 1)],
                                                rhs=OV[:, kt_, :], start=(n_ == 0), stop=(n_ == len(pts) - 1)),
                                                reads=[bPT[pt_], bC], writes=[bPSX])
                            post = None
                            if kt == kts[-1]:
                                def post(b, r=r, i=i, ob_box=ob_box, gi=gi):
                                    psx3 = PSX[:, :].rearrange("p (s c) -> p s c", c=128)
                                    kb.op(dve, lambda: nc.vector.tensor_scalar(out=RSC[:, :], in0=psx3[:, :, 64], scalar1=1e-30,
                                                                               scalar2=None, op0=ALU.max),
                                          reads=[bPSX], writes=[bRSC])
                                    kb.op(dve, lambda: nc.vector.reciprocal(out=RSC[:, :], in_=RSC[:, :]),
                                          reads=[bRSC], writes=[bRSC])
                                    for s4 in range(4):
                                        in1 = FB[:, 4 * i + s4, :] if r == 0 else IMP[:, s4, :]
                                        kb.op(dve, lambda s4=s4, in1=in1: nc.vector.scalar_tensor_tensor(
                                            out=IMP[:, s4, :], in0=PSX[:, 128 * s4:128 * s4 + 64], scalar=RSC[:, s4:s4 + 1],
                                            in1=in1, op0=ALU.mult, op1=ALU.add),
                                            reads=[bPSX, bRSC, bC, bIMP[s4]], writes=[bIMP[s4]])
                                    deferred.append((b + 2, fin_stage1(ob_box["ob"], r, 0, gi)))
                            add(pre=(tile_pre if first_blk else None), qk=qk, col=col, pv=pv, post=post)
                            first_blk = False

                    def topk(g=g, i=i, qs=qs, st=st):
                        for s4 in range(4):
                            kb.op(dve, lambda s4=s4: nc.vector.max(out=M8[:, s4, :], in_=IMP[:, s4, :]),
                                  reads=[bIMP[s4]], writes=[bM8[s4]])
                            kb.op(dve, lambda s4=s4: nc.vector.tensor_scalar(
                                out=NM[:, s4, 64:128], in0=IMP[:, s4, :], scalar1=M8[:, s4, 7:8], scalar2=1.0,
                                op0=ALU.is_ge, op1=ALU.subtract), reads=[bIMP[s4], bM8[s4]], writes=[bNM[s4]])
                        for s4 in range(4):
                            kb.op(pe, lambda s4=s4: nc.tensor.matmul(PSX[:, 128 * s4:128 * (s4 + 1)], lhsT=NM[:, s4, :],
                                                                     rhs=IDENTBIG[:], start=True, stop=True),
                                  reads=[bNM[s4], bC], writes=[bPSX])
                        for r in range(RPG):
                            kb.op(dve, lambda r=r: nc.vector.scalar_tensor_tensor(
                                out=QA[st][64:128, r, qs], in0=IOTA[64:128, :], scalar=-SLOPES[3 * g + r], in1=PSX[64:128, :],
                                op0=ALU.mult, op1=ALU.add), reads=[bC, bPSX], writes=[bQhi[st][r][i]])

                    def win_blocks(r):
                        h = 3 * g + r
                        vlist = [4, 5, 6, 7] if i == 0 else [3, 4, 0, 1, 2, 5, 6, 7]
                        ob_box = {}
                        for v in vlist:
                            kt = 4 * i - 4 + v
                            if v < 4:
                                c0, c1, m0, midx = 0, 128 * (v + 1), 128 * v, 4 + v
                            else:
                                c0, c1, m0, midx = 128 * (v - 4), NT, 128 * (v - 4), v - 4

                            def qk(p, kt=kt, r=r, midx=midx, qs=qs, i=i, st=st, c0=c0, c1=c1, m0=m0):
                                kb.op(pe, lambda: nc.tensor.matmul(
                                    PSs[p][:, c0:c1], lhsT=KW[st][0:65, 128 * kt:128 * (kt + 1)],
                                    rhs=QA[st][0:65, r, qs.start + c0:qs.start + c1],
                                    start=True, stop=False), reads=[bKWlo[st], bKWhi[st], bQlo[st][r], bQhi[st][r][i]],
                                    writes=[bPSs[p]])
                                kb.op(pe, lambda: nc.tensor.matmul(
                                    PSs[p][:, m0:m0 + 128], lhsT=IDENT[:], rhs=MASK[:, midx, m0:m0 + 128],
                                    start=False, stop=True), reads=[bC], writes=[bPSs[p]])
                            col = h * 35 + (v - 4 + 31)

                            def pv(pt, kt=kt, ob_box=ob_box, first=(v == vlist[0]), last=(v == vlist[-1]), st=st,
                                   c0=c0, c1=c1):
                                if first:
                                    ob_box["ob"] = nx("pso", 3)
                                ob = ob_box["ob"]
                                kb.op(pe, lambda: nc.tensor.matmul(
                                    PSo[ob][:, c0:c1], lhsT=VW[st][:, kt, :], rhs=PT[pt][:, c0:c1], start=first, stop=last),
                                    reads=[bVW[st], bVone[st], bPT[pt]], writes=[bPSo[ob]])
                            post = None
                            if v == vlist[-1]:
                                def post(b, r=r, ob_box=ob_box, gi=gi):
                                    deferred.append((b + 2, fin_stage1(ob_box["ob"], r, 2, gi)))
                            add(pre=None, qk=qk, col=col, pv=pv, post=post, cr=(c0, c1))

                    def sel_blocks(r, pre0):
                        h = 3 * g + r
                        nk = 4 * i + 4
                        kt_first = sel_first_tile(h, i)
                        ob_box = {}
                        for kt in range(kt_first, nk):
                            diag = kt >= 4 * i

                            c0 = 128 * (kt - 4 * i) if diag else 0

                            def qk(p, kt=kt, r=r, diag=diag, qs=qs, i=i, st=st, c0=c0):
                                kb.op(pe, lambda: nc.tensor.matmul(
                                    PSs[p][:, c0:NT], lhsT=KA[st][:, 128 * kt:128 * (kt + 1)],
                                    rhs=QA[st][:, r, qs.start + c0:qs.stop],
                                    start=True, stop=(not diag)), reads=[bKAlo[st], bKAhi[st], bQlo[st][r], bQhi[st][r][i]],
                                    writes=[bPSs[p]])
                                if diag:
                                    kb.op(pe, lambda: nc.tensor.matmul(
                                        PSs[p][:, c0:c0 + 128], lhsT=IDENT[:], rhs=MASK[:, kt - 4 * i, c0:c0 + 128],
                                        start=False, stop=True), reads=[bC], writes=[bPSs[p]])
                            col = h * 35 + (kt - 4 * i + 31)

                            def pv(pt, kt=kt, ob_box=ob_box, first=(kt == kt_first), last=(kt == nk - 1), st=st, c0=c0):
                                if first:
                                    ob_box["ob"] = nx("pso", 3)
                                ob = ob_box["ob"]
                                kb.op(pe, lambda: nc.tensor.matmul(
                                    PSo[ob][:, c0:NT], lhsT=VS[st][:, kt, :], rhs=PT[pt][:, c0:NT], start=first, stop=last),
                                    reads=[bVS[st], bVone[st], bPT[pt]], writes=[bPSo[ob]])
                            post = None
                            if kt == nk - 1:
                                def post(b, r=r, ob_box=ob_box, gi=gi, h=h, qs=qs, i=i):
                                    st2 = fin_stage1(ob_box["ob"], r, 1, gi)

                                    def st2b():
                                        st2()
                                        no = nx("nout", 2)
                                        kb.op(pool, lambda: nc.gpsimd.tensor_copy(out=NOUT[no][:], in_=NACC[:, r, :]),
                                              reads=[bNACC[r]], writes=[bNOUT[no]])
                                        kb.dma(pool, self.MIXT[CC + 64 * h:CC + 64 * (h + 1), qs], NOUT[no][:],
                                               reads=[bNOUT[no]], writes=[self.b_mix[i]])
                                    deferred.append((b + 2, st2b))
                            add(pre=(pre0 if kt == kt_first else None), qk=qk, col=col, pv=pv, post=post, cr=(c0, NT))

                    win_blocks(0)
                    sel_blocks(0, topk)
                    win_blocks(1)
                    sel_blocks(1, None)
                    win_blocks(2)
                    sel_blocks(2, None)
            run_stream()


def build(n_layers=DEPTH, phases=("a", "b", "c"), debug=(), ext_in=(), bkw={}):
    nc = bass.Bass("TRN2", target_bir_lowering=False)
    es = contextlib.ExitStack()
    with es:
        pg = Prog(nc, es, debug=debug, ext_in=ext_in)
        for l in range(n_layers):
            if "a" in phases:
                pg.phase_a(l)
            if "b0" in phases:
                pg.phase_b0(l)
            if "b12" in phases or "b" in phases:
                pg.phase_b12(l, **bkw)
            if "c" in phases:
                pg.phase_c(l, last=(l == n_layers - 1))
        pg.kb.barrier()
    return nc, pg


def make_in_maps(inputs):
    cst = host_constants()
    sm, fin, pe = host_small(inputs)
    x = np.asarray(inputs["x"])
    shared = {
        "w_in": np.asarray(inputs["w_in"]), "cmp_k_w1": np.asarray(inputs["cmp_k_w1"]),
        "cmp_k_w2": np.asarray(inputs["cmp_k_w2"]), "cmp_v_w1": np.asarray(inputs["cmp_v_w1"]),
        "cmp_v_w2": np.asarray(inputs["cmp_v_w2"]), "w_out": np.asarray(inputs["w_out"]),
        "w_gate_up": np.asarray(inputs["w_gate_up"]), "w_down": np.asarray(inputs["w_down"]),
        "c_small": sm, "c_fin": fin, "c_pe": pe,
    }
    shared.update(cst)
    maps = []
    for b in range(x.shape[0]):
        m = dict(shared)
        m["xT"] = np.ascontiguousarray(x[b].T)
        maps.append(m)
    return maps


def kernel(**inputs):
    nc, pg = build()
    maps = make_in_maps(inputs)
    res = run_bass_kernel_spmd(nc, maps, core_ids=list(range(len(maps))))
    out = np.stack([np.ascontiguousarray(r["outT"].T) for r in res.results], axis=0)
    return out.astype(np.float32)
```
